# Optimizing a Trainium2 kernel written in Bass

```python
import math
import jax, jax.numpy as jnp
from jax import lax
import numpy as np

D_MODEL = 1024
BATCH = 4
SEQ = 4096
DEPTH = 1

SG_GROUPS = 8
SG_GROUP_DIM = 64
SG_WIDTH = SG_GROUPS * SG_GROUP_DIM
CHUNK = 128
DA_HEADS = 8
DA_HEAD_DIM = 64
DA_V_DIM = 2 * DA_HEAD_DIM
DA_QK_WIDTH = DA_HEADS * 2 * DA_HEAD_DIM
DA_WIDTH = DA_HEADS * DA_V_DIM
Q_BLOCK = 128
N_BRANCHES = 2
IN_COLS = 2 * SG_WIDTH + 2 * DA_QK_WIDTH + DA_WIDTH + N_BRANCHES * D_MODEL
SPLITS = (SG_WIDTH, 2 * SG_WIDTH, 2 * SG_WIDTH + DA_QK_WIDTH,
          2 * SG_WIDTH + 2 * DA_QK_WIDTH, 2 * SG_WIDTH + 2 * DA_QK_WIDTH + DA_WIDTH)
D_FF = -(-(8 * D_MODEL) // (3 * 256)) * 256
EPS = 1e-6

kernel_name = "hybrid_gmlp_diffattn_gated_encoder"


def lambda_init(layer_idx):
    return 0.8 - 0.6 * math.exp(-0.3 * layer_idx)


def rmsnorm(x, g):
    xf = x.astype(jnp.float32)
    y = xf * lax.rsqrt(jnp.mean(xf * xf, axis=-1, keepdims=True) + EPS)
    return (y * g.astype(jnp.float32)).astype(x.dtype)


def layernorm(x, g, b):
    xf = x.astype(jnp.float32)
    mu = jnp.mean(xf, axis=-1, keepdims=True)
    var = jnp.mean(jnp.square(xf - mu), axis=-1, keepdims=True)
    y = (xf - mu) * lax.rsqrt(var + EPS)
    return (y * g.astype(jnp.float32) + b.astype(jnp.float32)).astype(x.dtype)


def alibi_slopes(n_heads):
    return jnp.asarray(2.0 ** (-8.0 * (np.arange(n_heads) + 1) / n_heads), dtype=jnp.float32)


def spatial_gating(u, v, ln_g, ln_b, w_s, b_s):
    B, S, _ = u.shape
    u = jax.nn.gelu(u)
    v = layernorm(jax.nn.gelu(v), ln_g, ln_b)
    vc = v.reshape(B, S // CHUNK, CHUNK, SG_GROUPS, SG_GROUP_DIM)
    sv = jnp.einsum('gts,bcsge->bctge', w_s, vc) + b_s.T[:, :, None]
    return u * sv.reshape(B, S, SG_WIDTH)


def diff_attention(q, k, v, qn_g, kn_g, lq1, lk1, lq2, lk2, subln_g, lam_init):
    B, S, _ = q.shape
    H, dh = DA_HEADS, DA_HEAD_DIM
    q = rmsnorm(q.reshape(B, S, H, 2, dh), qn_g).transpose(0, 2, 3, 1, 4)
    k = rmsnorm(k.reshape(B, S, H, 2, dh), kn_g).transpose(0, 2, 3, 1, 4)
    v = v.reshape(B, S, H, DA_V_DIM).transpose(0, 2, 1, 3)
    f32 = jnp.float32
    lam = (jnp.exp(jnp.sum(lq1.astype(f32) * lk1.astype(f32)))
           - jnp.exp(jnp.sum(lq2.astype(f32) * lk2.astype(f32))) + lam_init)
    slopes = alibi_slopes(H)
    scale = 1.0 / math.sqrt(dh)
    kpos = jnp.arange(S, dtype=jnp.int32)
    n_blocks = S // Q_BLOCK

    def one_block(args):
        qb, start = args
        qpos = start + jnp.arange(Q_BLOCK, dtype=jnp.int32)
        s = jnp.einsum('bhcqd,bhckd->bhcqk', qb, k).astype(f32) * scale
        dist = jnp.abs(qpos[:, None] - kpos[None, :]).astype(f32)
        s = s - slopes[None, :, None, None, None] * dist
        p = jax.nn.softmax(s, axis=-1)
        a = p[:, :, 0] - lam * p[:, :, 1]
        return jnp.einsum('bhqk,bhkd->bhqd', a.astype(v.dtype), v)

    q_blocks = q.reshape(B, H, 2, n_blocks, Q_BLOCK, dh).transpose(3, 0, 1, 2, 4, 5)
    starts = jnp.arange(n_blocks, dtype=jnp.int32) * Q_BLOCK
    o = lax.map(one_block, (q_blocks, starts))
    o = o.transpose(1, 0, 3, 2, 4).reshape(B, S, H, DA_V_DIM)
    o = rmsnorm(o, subln_g) * (1.0 - lam_init)
    return o.reshape(B, S, DA_WIDTH)


def setup_inputs(seed: int = 0) -> dict:
    key = jax.random.key(seed)
    ks = jax.random.split(key, 24)
    nrm = lambda k, shape, s: jax.random.normal(k, shape, jnp.float32) * s
    L, D = DEPTH, D_MODEL
    return {
        "x": nrm(ks[0], (BATCH, SEQ, D), 1.0),
        "norm1_g": 1.0 + nrm(ks[1], (L, D), 0.02),
        "w_in": nrm(ks[2], (L, D, IN_COLS), D ** -0.5),
        "b_gate": nrm(ks[3], (L, N_BRANCHES * D), 0.02),
        "sg_ln_g": 1.0 + nrm(ks[4], (L, SG_WIDTH), 0.02),
        "sg_ln_b": nrm(ks[5], (L, SG_WIDTH), 0.02),
        "sg_w": nrm(ks[6], (L, SG_GROUPS, CHUNK, CHUNK), CHUNK ** -0.5),
        "sg_b": 1.0 + nrm(ks[7], (L, SG_GROUPS, CHUNK), 0.02),
        "q_norm_g": 1.0 + nrm(ks[8], (L, DA_HEAD_DIM), 0.02),
        "k_norm_g": 1.0 + nrm(ks[9], (L, DA_HEAD_DIM), 0.02),
        "lam_q1": nrm(ks[10], (L, DA_HEAD_DIM), 0.1),
        "lam_k1": nrm(ks[11], (L, DA_HEAD_DIM), 0.1),
        "lam_q2": nrm(ks[12], (L, DA_HEAD_DIM), 0.1),
        "lam_k2": nrm(ks[13], (L, DA_HEAD_DIM), 0.1),
        "subln_g": 1.0 + nrm(ks[14], (L, DA_V_DIM), 0.02),
        "w_proj_sg": nrm(ks[15], (L, SG_WIDTH, D), SG_WIDTH ** -0.5),
        "w_proj_da": nrm(ks[16], (L, DA_WIDTH, D), DA_WIDTH ** -0.5),
        "w_out": nrm(ks[17], (L, D, D), D ** -0.5),
        "norm2_g": 1.0 + nrm(ks[18], (L, D), 0.02),
        "w_ffn_gate": nrm(ks[19], (L, D, D_FF), D ** -0.5),
        "w_ffn_up": nrm(ks[20], (L, D, D_FF), D ** -0.5),
        "w_ffn_down": nrm(ks[21], (L, D_FF, D), D_FF ** -0.5),
    }


def reference(x, norm1_g, w_in, b_gate, sg_ln_g, sg_ln_b, sg_w, sg_b, q_norm_g, k_norm_g,
              lam_q1, lam_k1, lam_q2, lam_k2, subln_g, w_proj_sg, w_proj_da, w_out,
              norm2_g, w_ffn_gate, w_ffn_up, w_ffn_down):
    for l in range(DEPTH):
        xn = rmsnorm(x, norm1_g[l])
        proj = xn @ w_in[l]
        u, v_sg, q, k, v_da, gate_pre = jnp.split(proj, SPLITS, axis=-1)
        o_sg = spatial_gating(u, v_sg, sg_ln_g[l], sg_ln_b[l], sg_w[l], sg_b[l])
        o_da = diff_attention(q, k, v_da, q_norm_g[l], k_norm_g[l], lam_q1[l], lam_k1[l],
                              lam_q2[l], lam_k2[l], subln_g[l], lambda_init(l))
        y_sg = o_sg @ w_proj_sg[l]
        y_da = o_da @ w_proj_da[l]
        gates = jax.nn.sigmoid((gate_pre + b_gate[l]).astype(jnp.float32)).astype(x.dtype)
        g_sg, g_da = jnp.split(gates, N_BRANCHES, axis=-1)
        x = x + (g_sg * y_sg + g_da * y_da) @ w_out[l]
        hn = rmsnorm(x, norm2_g[l])
        x = x + (jax.nn.silu(hn @ w_ffn_gate[l]) * (hn @ w_ffn_up[l])) @ w_ffn_down[l]
    return x
```

```python
import os
import math
import numpy as np
import ml_dtypes
from contextlib import ExitStack
import concourse.bass as bass
import concourse.mybir as mybir
from concourse.bass_utils import run_bass_kernel_spmd

F32 = mybir.dt.float32
BF16 = mybir.dt.bfloat16
AF = mybir.ActivationFunctionType
ALU = mybir.AluOpType

EPS = 1e-6
D = 1024
S_OWN = 2048
S_ALL = 4096
NT_OWN = 16
NT_ALL = 32
H = 8
DFF = 2816
LAM_INIT = 0.8 - 0.6 * math.exp(-0.3 * 0)
FBLOCKS = [(0, 6), (6, 6), (12, 5), (17, 5)]


class Sched:
    ENGS = ("pe", "act", "dve", "pool", "sp")

    def __init__(self):
        self.q = {e: [] for e in self.ENGS}
        self.ncomp = {e: 0 for e in self.ENGS}
        self.lastw = {}
        self.readers = {}
        self.dma_cnt = {}
        self.final_tokens = []
        self.pending = {e: None for e in self.ENGS}

    def barrier(self):
        toks = {}
        for e in self.ENGS:
            if self.ncomp[e] > 0:
                toks[("eng", e)] = self.ncomp[e]
        for k, v in self.dma_cnt.items():
            toks[("dma", k)] = v
        for e in self.ENGS:
            self.pending[e] = dict(toks)

    def add(self, eng, fn, reads=(), writes=(), dma_sem=None, final=False):
        waits = {}

        def need(tok):
            if tok is None:
                return
            s, v, teng = tok
            if teng == eng and eng == "pe" and dma_sem is None:
                return
            if waits.get(s, 0) < v:
                waits[s] = v

        if self.pending[eng]:
            for s, v in self.pending[eng].items():
                if s == ("eng", "pe") and eng == "pe":
                    continue
                waits[s] = max(waits.get(s, 0), v)
            self.pending[eng] = None
        for k in reads:
            need(self.lastw.get(k))
        for k in writes:
            need(self.lastw.get(k))
            for s, (v, teng) in self.readers.get(k, {}).items():
                need((s, v, teng))
        if dma_sem is None:
            self.ncomp[eng] += 1
            tok = (("eng", eng), self.ncomp[eng], eng)
        else:
            self.dma_cnt[dma_sem] = self.dma_cnt.get(dma_sem, 0) + 16
            tok = (("dma", dma_sem), self.dma_cnt[dma_sem], None)
        for k in reads:
            d = self.readers.setdefault(k, {})
            if d.get(tok[0], (0, None))[0] < tok[1]:
                d[tok[0]] = (tok[1], tok[2])
        for k in writes:
            self.lastw[k] = tok
            self.readers[k] = {}
        self.q[eng].append((fn, waits, tok))
        if final:
            self.final_tokens.append(tok)
        return tok

    def emit(self, nc, stack):
        sems = {}

        def sem(key):
            if key not in sems:
                sems[key] = stack.enter_context(nc.semaphore("s%d" % len(sems)))
            return sems[key]

        for e in self.ENGS:
            for fn, waits, tok in self.q[e]:
                sem(tok[0])
        engobj = {"pe": "tensor", "act": "scalar", "dve": "vector", "pool": "gpsimd", "sp": "sync"}
        finals = list(self.final_tokens)
        with nc.Block() as block:
            for e in self.ENGS:
                ops = self.q[e]
                if not ops:
                    continue

                def body(eng, ops=ops, e=e):
                    waited = {}
                    for fn, waits, tok in ops:
                        pend = []
                        for s, v in waits.items():
                            if waited.get(s, 0) >= v:
                                continue
                            waited[s] = v
                            pend.append((s, v))
                        attach = None
                        if pend and tok[0][0] != "dma":
                            attach = pend.pop()
                        for s, v in pend:
                            eng.wait_ge(sem(s), v)
                        ins = fn(eng)
                        if attach is not None:
                            ins._wait_ge(sem(attach[0]), attach[1])
                        ins.then_inc(sem(tok[0]), 16 if tok[0][0] == "dma" else 1)
                    if e == "sp":
                        for tok in finals:
                            if waited.get(tok[0], 0) < tok[1]:
                                waited[tok[0]] = tok[1]
                                eng.wait_ge(sem(tok[0]), tok[1])

                getattr(block, engobj[e])(body)
        return len(sems)


def build_program(phases="ABC", dbg=False):
    nc = bass.Bass("TRN2", target_bir_lowering=False)

    def din(name, shape, dt=F32):
        return nc.dram_tensor(name, shape, dt, kind="ExternalInput").ap()

    x = din("x", [S_ALL, D])
    norm1_g = din("norm1_g", [1, D])
    w_in = din("w_in", [D, 6144])
    b_gate = din("b_gate", [1, 2048])
    sg_ln_g = din("sg_ln_g", [1, 512])
    sg_ln_b = din("sg_ln_b", [1, 512])
    sg_wT = din("sg_wT", [128, 8, 128])
    sg_b = din("sg_b", [8, 128])
    q_norm_g = din("q_norm_g", [1, 64])
    k_norm_g = din("k_norm_g", [1, 64])
    lam_q1 = din("lam_q1", [1, 64])
    lam_k1 = din("lam_k1", [1, 64])
    lam_q2 = din("lam_q2", [1, 64])
    lam_k2 = din("lam_k2", [1, 64])
    subln_g = din("subln_g", [1, 128])
    w_proj_sg = din("w_proj_sg", [512, D])
    w_proj_da = din("w_proj_da", [D, D])
    w_out = din("w_out", [D, D])
    norm2_g = din("norm2_g", [1, D])
    w_ffn_gate = din("w_ffn_gate", [D, DFF])
    w_ffn_up = din("w_ffn_up", [D, DFF])
    w_ffn_down = din("w_ffn_down", [DFF, D])
    c_ident = din("c_ident", [128, 128], BF16)
    c_ones64 = din("c_ones64", [128, 128], BF16)
    c_onesrow = din("c_onesrow", [1, 512], BF16)
    c_e8 = din("c_e8", [8, 512], BF16)
    c_diag = din("c_diag", [128, 128], BF16)
    c_kaug = din("c_kaug", [4, S_ALL], BF16)
    c_qaugL = din("c_qaugL", [4, S_OWN], BF16)
    c_qaugR = din("c_qaugR", [4, S_OWN], BF16)

    out = nc.dram_tensor("out", [S_OWN, D], F32, kind="ExternalOutput").ap()
    skind = "ExternalOutput" if dbg else "Internal"
    ZSG = nc.dram_tensor("zsg_scr", [S_OWN, D], BF16, kind=skind).ap()
    GDA = nc.dram_tensor("gda_scr", [S_OWN, D], BF16, kind=skind).ap()
    ODA = nc.dram_tensor("oda_dbg", [S_OWN, D], BF16, kind=skind).ap() if dbg else None

    w_in_v = w_in.rearrange("(kc p) n -> p kc n", p=128)

    S = Sched()
    with ExitStack() as st:
        def sbt(name, shape, dt):
            return st.enter_context(nc.sbuf_tensor(name, shape, dt))

        A64 = sbt("A64", [128, 32768], BF16)
        WA = sbt("WA", [128, 37888], BF16)
        B32 = sbt("B32", [128, 16384], BF16)
        TA = sbt("TA", [128, 3584], F32)
        B32f = B32[:].bitcast(F32)
        TAb = TA[:].bitcast(BF16)
        xnT = A64[:].rearrange("p (k t) -> p k t", k=8)
        hacc = A64[:].bitcast(F32).rearrange("p (t d) -> p t d", t=16)
        oda = B32[:].rearrange("p (t d) -> p t d", t=16)
        hnT = B32[:].rearrange("p (t k n) -> p t k n", t=16, k=8)

        def wa(off, n):
            return WA[:, off:off + n]

        PSP = [st.enter_context(nc.psum_tensor("psp%d" % i, [128, 2, 512], F32)) for i in range(4)]

        def bank(i):
            return PSP[i // 2][:, i % 2, :]

        def bk(i):
            return ("ps", i)

        ident = sbt("ident", [128, 128], BF16)
        ones64 = sbt("ones64", [128, 128], BF16)
        onesrow = sbt("onesrow", [1, 512], BF16)
        diag = sbt("diag", [128, 128], BF16)
        g1T = sbt("g1T", [128, 8], F32)
        g2T = sbt("g2T", [128, 8], F32)
        lng_b = B32f[:, 1024:1536]
        lnb_b = B32f[:, 1536:2048]
        bs8 = TAb[0:8, 0:128]
        e8 = sbt("e8", [8, 512], BF16)
        bgate_row = TAb[0:1, 1024:3072]
        epsb = sbt("epsb", [128, 1], F32)
        gq_b = sbt("gq_b", [128, 64], F32)
        gk_b = sbt("gk_b", [128, 64], F32)
        gqT = sbt("gqT", [128, 1], F32)
        gkT = sbt("gkT", [128, 1], F32)
        gqs = sbt("gqs", [128, 8], F32)
        lam4 = sbt("lam4", [128, 4, 64], F32)
        lamt = sbt("lamt", [128, 2, 64], F32)
        lams = sbt("lams", [128, 2], F32)
        neglam = sbt("neglam", [128, 1], F32)
        negc = sbt("negc", [128, 1], F32)
        cmax = sbt("cmax", [128, 2], F32)
        sublng = sbt("sublng", [128, 128], F32)

        def ld(dst_ap, src_ap, key, eng="sp", **kw):
            S.add(eng, lambda e: e.dma_start(out=dst_ap, in_=src_ap, **kw), writes=[key], dma_sem=key)

        ld(ident[:], c_ident[:, :], "ident")
        ld(onesrow[:], c_onesrow[:, :], "onesrow")
        ld(g1T[:], norm1_g[0].rearrange("(kc p) -> p kc", p=128), "g1T", allow_slow_non_contiguous=True)
        ld(g2T[:], norm2_g[0].rearrange("(kc p) -> p kc", p=128), "g2T", allow_slow_non_contiguous=True)
        ld(lng_b, sg_ln_g[0:1, :].broadcast_to([128, 512]), "lng_b")
        ld(lnb_b, sg_ln_b[0:1, :].broadcast_to([128, 512]), "lnb_b")
        ld(bs8, sg_b[:, :], "bs8", eng="pool")
        ld(e8[:], c_e8[:, :], "e8")
        ld(bgate_row, b_gate[:, :], "bgate_row", eng="pool")
        S.add("dve", lambda e: e.memset(epsb[:], EPS), writes=["epsb"])
        XT = sbt("XT", [128, 2080], F32)
        xt = [XT[:, 0:1024], XT[:, 1024:2048]]
        ss = [sbt("ss%d" % i, [128, 1], F32) for i in range(2)]
        xs = [sbt("xs%d" % i, [128, D], BF16) for i in range(2)]
        vt = [B32[:, 14336 + i * 1024:14336 + (i + 1) * 1024] for i in range(2)]

        def rms_stage1(src_ap, src_key, b):
            S.add("act", lambda e: e.activation(out=xs[b][:], in_=src_ap, func=AF.Square, scale=1.0 / 32, accum_out=ss[b][:]),
                  reads=[src_key], writes=[("xs", b), ("ss", b)])
            S.add("act", lambda e: e.activation(out=ss[b][:], in_=ss[b][:], func=AF.Sqrt, bias=epsb[:, 0:1], scale=1.0),
                  reads=[("ss", b), "epsb"], writes=[("ss", b)])
            S.add("dve", lambda e: e.reciprocal(out=ss[b][:], in_=ss[b][:]), reads=[("ss", b)], writes=[("ss", b)])
            S.add("dve", lambda e: e.tensor_scalar(out=xs[b][:], in0=src_ap, scalar1=ss[b][:, 0:1], scalar2=None, op0=ALU.mult),
                  reads=[src_key, ("ss", b)], writes=[("xs", b)])

        def rms_stage2(gT, gkey, dst_ap, dst_key, b):
            for kc in range(8):
                S.add("pe", lambda e, kc=kc: e.matmul(bank(kc // 4)[:, (kc % 4) * 128:(kc % 4 + 1) * 128], lhsT=xs[b][:, kc * 128:(kc + 1) * 128], rhs=ident[:], start=True, stop=True),
                      reads=[("xs", b), "ident"], writes=[bk(kc // 4)])
            S.add("dve", lambda e: e.tensor_tensor(out=dst_ap, in0=PSP[0][:].rearrange("p a (b t) -> p (a b) t", t=128),
                                                   in1=gT[:].unsqueeze(2).broadcast_to([128, 8, 128]), op=ALU.mult),
                  reads=[bk(0), bk(1), gkey], writes=[dst_key])

        wqk = [wa(24576, 2048).rearrange("p (k n) -> p k n", k=8), wa(20544, 2048).rearrange("p (k n) -> p k n", k=8)]
        Wvh = [wa(26624, 1024).rearrange("p (k n) -> p k n", k=8), wa(22592, 1024).rearrange("p (k n) -> p k n", k=8)]
        Vhb = [wa(27648, 4160).rearrange("p (k n) -> p k n", k=32), wa(16384, 4160).rearrange("p (k n) -> p k n", k=32),
               XT[:].bitcast(BF16)[:, 0:4160].rearrange("p (k n) -> p k n", k=32)]
        Wvp = wa(35840, 2048).rearrange("p (k n) -> p k n", k=8)

        def load_wv(h):
            ld(Wvh[h % 2], w_in_v[:, :, 3072 + h * 128:3072 + (h + 1) * 128], ("Wvh", h % 2), eng="pool")

        def load_wqk(h):
            wb = h % 2
            ld(wqk[wb][:, :, 0:128], w_in_v[:, :, 1024 + h * 128:1024 + (h + 1) * 128], ("wq", wb), eng="pool")
            ld(wqk[wb][:, :, 128:256], w_in_v[:, :, 2048 + h * 128:2048 + (h + 1) * 128], ("wk", wb), eng="pool")

        if "A" in phases:
            Wuv = wa(0, 8192).rearrange("p (k n) -> p k n", k=8)
            Wgt = wa(8192, 16384).rearrange("p (k n) -> p k n", k=8)
            PA = wa(32768, 4096).rearrange("p (k n) -> p k n", k=4)
            WsT = wa(36864, 1024).rearrange("p (g t) -> p g t", g=8)
            load_wv(0)
            S.add("pool", lambda e: e.memset(Vhb[0][:, :, 128:130], 1.0), writes=[("Vones", 0)])
            load_wqk(0)
            ld(Wuv, w_in_v[:, :, 0:1024], "Wuv", eng="pool")
            ld(WsT, sg_wT[:, :, :], "WsT", eng="pool")
            ld(PA, w_proj_sg.rearrange("(kc p) n -> p kc n", p=128), "PA", eng="pool")
            for j in range(2):
                ld(Wgt[:, :, j * 1024:(j + 1) * 1024], w_in_v[:, :, 4096 + j * 1024:4096 + (j + 1) * 1024], ("Wgt", j), eng="pool")
            vg = [B32f[:, i * 512:(i + 1) * 512] for i in range(2)]
            ug = [B32[:, 4096 + i * 512:4096 + (i + 1) * 512] for i in range(2)]
            vln = [B32[:, 5120 + i * 512:5120 + (i + 1) * 512] for i in range(2)]
            osg = [B32[:, 6144 + i * 512:6144 + (i + 1) * 512] for i in range(2)]
            osgT = [B32[:, 7168 + i * 512:7168 + (i + 1) * 512].rearrange("p (k t) -> p k t", k=4) for i in range(2)]
            gsg = [B32[:, 8192 + i * 1024:8192 + (i + 1) * 1024] for i in range(2)]
            gda_t = [B32[:, 10240 + i * 1024:10240 + (i + 1) * 1024] for i in range(2)]
            zsg_t = [B32[:, 12288 + i * 1024:12288 + (i + 1) * 1024] for i in range(2)]
            lnst = [sbt("lnst%d" % i, [128, 4], F32) for i in range(2)]

            def front_load(t, sl):
                ld(xt[sl][:], x[t * 128:(t + 1) * 128, :], ("xt", sl))

            def front_compute(t, sl):
                S.add("act", lambda e: e.activation(out=xs[sl][:], in_=xt[sl][:], func=AF.Square, scale=1.0 / 32, accum_out=ss[sl][:]),
                      reads=[("xt", sl)], writes=[("xs", sl), ("ss", sl)])
                S.add("act", lambda e: e.activation(out=ss[sl][:], in_=ss[sl][:], func=AF.Sqrt, bias=epsb[:, 0:1], scale=1.0),
                      reads=[("ss", sl), "epsb"], writes=[("ss", sl)])
                S.add("dve", lambda e: e.reciprocal(out=ss[sl][:], in_=ss[sl][:]), reads=[("ss", sl)], writes=[("ss", sl)])
                S.add("dve", lambda e: e.tensor_scalar(out=xs[sl][:], in0=xt[sl][:], scalar1=ss[sl][:, 0:1], scalar2=None, op0=ALU.mult),
                      reads=[("xt", sl), ("ss", sl)], writes=[("xs", sl)])

            def tile_gen(t, sl, nxt_t):
                own = t < NT_OWN
                b0, b1, b2, b3 = 4 * sl, 4 * sl + 1, 4 * sl + 2, 4 * sl + 3
                P01 = PSP[2 * sl]
                P23 = PSP[2 * sl + 1]
                tcols = slice(t * 128, (t + 1) * 128)
                if nxt_t is not None:
                    front_load(nxt_t, sl)
                for kc in range(8):
                    S.add("pe", lambda e, kc=kc: e.matmul(bank(b0 + kc // 4)[:, (kc % 4) * 128:(kc % 4 + 1) * 128], lhsT=xs[sl][:, kc * 128:(kc + 1) * 128], rhs=ident[:], start=True, stop=True),
                          reads=[("xs", sl), "ident"], writes=[bk(b0 + kc // 4)])
                yield
                S.add("dve", lambda e: e.tensor_tensor(out=xnT[:, :, tcols], in0=P01[:].rearrange("p a (b t) -> p (a b) t", t=128),
                                                       in1=g1T[:].unsqueeze(2).broadcast_to([128, 8, 128]), op=ALU.mult),
                      reads=[bk(b0), bk(b1), "g1T"], writes=[("xnT", t)])
                if nxt_t is not None:
                    front_compute(nxt_t, sl)
                yield
                for kc in range(8):
                    S.add("pe", lambda e, kc=kc: e.matmul(bank(b2)[:, 0:128], lhsT=xnT[:, kc, tcols], rhs=Wvh[0][:, kc, :], start=(kc == 0), stop=(kc == 7)),
                          reads=[("xnT", t), ("Wvh", 0)], writes=[bk(b2)])
                yield
                S.add("dve", lambda e: e.tensor_copy(out=Vhb[0][:, t, 0:128], in_=bank(b2)[:, 0:128]), reads=[bk(b2)], writes=[("Vh", 0, t)])
                if not own:
                    return
                for j in range(2):
                    for kc in range(8):
                        S.add("pe", lambda e, kc=kc, j=j: e.matmul(bank(b0 + j), lhsT=xnT[:, kc, tcols], rhs=Wuv[:, kc, j * 512:(j + 1) * 512], start=(kc == 0), stop=(kc == 7)),
                              reads=[("xnT", t), "Wuv"], writes=[bk(b0 + j)])
                yield
                S.add("act", lambda e: e.activation(out=ug[sl][:], in_=bank(b0), func=AF.Gelu_apprx_tanh), reads=[bk(b0)], writes=[("ug", sl)])
                S.add("act", lambda e: e.activation(out=vg[sl][:], in_=bank(b1), func=AF.Gelu_apprx_tanh, accum_out=lnst[sl][:, 0:1]),
                      reads=[bk(b1)], writes=[("vg", sl), ("lnst", sl, 0)])
                yield
                for j in range(2):
                    for kc in range(8):
                        S.add("pe", lambda e, kc=kc, j=j: e.matmul(bank(b2 + j), lhsT=xnT[:, kc, tcols], rhs=Wgt[:, kc, j * 512:(j + 1) * 512], start=(kc == 0), stop=False),
                              reads=[("xnT", t), ("Wgt", 0)], writes=[bk(b2 + j)])
                    S.add("pe", lambda e, j=j: e.matmul(bank(b2 + j), lhsT=onesrow[0:1, 0:128], rhs=bgate_row[0:1, j * 512:(j + 1) * 512], start=False, stop=True),
                          reads=["onesrow", "bgate_row"], writes=[bk(b2 + j)])
                S.add("dve", lambda e: e.tensor_scalar(out=lnst[sl][:, 1:2], in0=lnst[sl][:, 0:1], scalar1=-1.0 / 512, scalar2=None, op0=ALU.mult),
                      reads=[("lnst", sl, 0)], writes=[("lnst", sl, 1)])
                yield
                S.add("act", lambda e: e.activation(out=vln[sl][:], in_=vg[sl][:], func=AF.Square, bias=lnst[sl][:, 1:2], scale=1.0, accum_out=lnst[sl][:, 2:3]),
                      reads=[("vg", sl), ("lnst", sl, 1)], writes=[("vln", sl), ("lnst", sl, 2)])
                S.add("act", lambda e: e.activation(out=lnst[sl][:, 2:3], in_=lnst[sl][:, 2:3], func=AF.Sqrt, bias=epsb[:, 0:1], scale=1.0 / 512),
                      reads=[("lnst", sl, 2), "epsb"], writes=[("lnst", sl, 2)])
                yield
                S.add("dve", lambda e: e.reciprocal(out=lnst[sl][:, 2:3], in_=lnst[sl][:, 2:3]), reads=[("lnst", sl, 2)], writes=[("lnst", sl, 2)])
                S.add("dve", lambda e: e.tensor_scalar(out=vg[sl][:], in0=vg[sl][:], scalar1=lnst[sl][:, 1:2], scalar2=lnst[sl][:, 2:3], op0=ALU.add, op1=ALU.mult),
                      reads=[("vg", sl), ("lnst", sl, 1), ("lnst", sl, 2)], writes=[("vg", sl)])
                S.add("dve", lambda e: e.tensor_tensor(out=vg[sl][:], in0=vg[sl][:], in1=lng_b[:], op=ALU.mult), reads=[("vg", sl), "lng_b"], writes=[("vg", sl)])
                S.add("dve", lambda e: e.tensor_tensor(out=vln[sl][:], in0=vg[sl][:], in1=lnb_b[:], op=ALU.add), reads=[("vg", sl), "lnb_b"], writes=[("vln", sl)])
                S.add("act", lambda e: e.activation(out=gsg[sl][:], in_=P23[:].rearrange("p a n -> p (a n)"), func=AF.Sigmoid),
                      reads=[bk(b2), bk(b3)], writes=[("gsg", sl)])
                yield
                S.add("pe", lambda e: e.matmul(bank(b0), lhsT=bs8, rhs=e8[:], start=True, stop=False, skip_group_check=True),
                      reads=["bs8", "e8"], writes=[bk(b0)])
                for g in range(8):
                    S.add("pe", lambda e, g=g: e.matmul(bank(b0)[:, g * 64:(g + 1) * 64], lhsT=WsT[:, g, :], rhs=vln[sl][:, g * 64:(g + 1) * 64], start=False, stop=(g == 7), skip_group_check=True),
                          reads=["WsT", ("vln", sl)], writes=[bk(b0)])
                for j in range(2):
                    for kc in range(8):
                        S.add("pe", lambda e, kc=kc, j=j: e.matmul(bank(b2 + j), lhsT=xnT[:, kc, tcols], rhs=Wgt[:, kc, 1024 + j * 512:1024 + (j + 1) * 512], start=(kc == 0), stop=False),
                              reads=[("xnT", t), ("Wgt", 1)], writes=[bk(b2 + j)])
                    S.add("pe", lambda e, j=j: e.matmul(bank(b2 + j), lhsT=onesrow[0:1, 0:128], rhs=bgate_row[0:1, 1024 + j * 512:1024 + (j + 1) * 512], start=False, stop=True),
                          reads=["onesrow", "bgate_row"], writes=[bk(b2 + j)])
                yield
                S.add("dve", lambda e: e.tensor_tensor(out=osg[sl][:], in0=bank(b0), in1=ug[sl][:], op=ALU.mult), reads=[bk(b0), ("ug", sl)], writes=[("osg", sl)])
                S.add("act", lambda e: e.activation(out=gda_t[sl][:], in_=P23[:].rearrange("p a n -> p (a n)"), func=AF.Sigmoid),
                      reads=[bk(b2), bk(b3)], writes=[("gda_t", sl)])
                S.add("pool", lambda e: e.dma_start(out=GDA[t * 128:(t + 1) * 128, :], in_=gda_t[sl][:]),
                      reads=[("gda_t", sl)], writes=[("GDA", t)], dma_sem=("gda_st", sl))
                yield
                for kc in range(4):
                    S.add("pe", lambda e, kc=kc: e.matmul(bank(b1)[:, kc * 128:(kc + 1) * 128], lhsT=osg[sl][:, kc * 128:(kc + 1) * 128], rhs=ident[:], start=True, stop=True),
                          reads=[("osg", sl), "ident"], writes=[bk(b1)])
                yield
                S.add("act", lambda e: e.copy(out=osgT[sl][:], in_=bank(b1).rearrange("p (k t) -> p k t", k=4)), reads=[bk(b1)], writes=[("osgT", sl)])
                yield
                for j in range(2):
                    for kc in range(4):
                        S.add("pe", lambda e, kc=kc, j=j: e.matmul(bank(b2 + j), lhsT=osgT[sl][:, kc, :], rhs=PA[:, kc, j * 512:(j + 1) * 512], start=(kc == 0), stop=(kc == 3)),
                              reads=[("osgT", sl), "PA"], writes=[bk(b2 + j)])
                yield
                S.add("dve", lambda e: e.tensor_tensor(out=zsg_t[sl][:], in0=P23[:].rearrange("p a n -> p (a n)"), in1=gsg[sl][:], op=ALU.mult),
                      reads=[bk(b2), bk(b3), ("gsg", sl)], writes=[("zsg_t", sl)])
                S.add("pool", lambda e: e.dma_start(out=ZSG[t * 128:(t + 1) * 128, :], in_=zsg_t[sl][:]),
                      reads=[("zsg_t", sl)], writes=[("ZSG", t)], dma_sem=("zsg_st", sl))

            order = list(range(NT_OWN, NT_ALL)) + list(range(NT_OWN))
            slot_tiles = [order[0::2], order[1::2]]

            def slot_gen(sl):
                tiles = slot_tiles[sl]
                for i, t in enumerate(tiles):
                    yield from tile_gen(t, sl, tiles[i + 1] if i + 1 < len(tiles) else None)

            for sl in range(2):
                front_load(slot_tiles[sl][0], sl)
                front_compute(slot_tiles[sl][0], sl)
            active = [slot_gen(0), slot_gen(1)]
            step = 0
            STAGGER = 5
            while any(a is not None for a in active):
                for sl in range(2):
                    if active[sl] is not None and (sl == 0 or step >= STAGGER):
                        try:
                            next(active[sl])
                        except StopIteration:
                            active[sl] = None
                step += 1
            S.barrier()

        if "B" in phases:
            kT = wa(0, 8192).rearrange("p (c t) -> p c t", c=2)
            qTL = wa(8192, 4096).rearrange("p (c t) -> p c t", c=2)
            qTR = wa(12288, 4096).rearrange("p (c t) -> p c t", c=2)
            PT = [wa(32768 + i * 1024, 1024).rearrange("p (c n) -> p c n", c=2) for i in range(3)]
            rk = [TA[:, i * 512:(i + 1) * 512] for i in range(2)]
            o4 = TA[:, 1024:1536].rearrange("p (q d) -> p q d", q=4)
            sq = [TAb[:, 3072 + i * 512:3072 + (i + 1) * 512] for i in range(2)]
            accs = TA[:, 2048:3488].rearrange("p (s d) -> p s d", s=9)
            rec9 = sbt("rec9", [128, 8], F32)
            ssall = sbt("ssall", [128, 16, 8], F32)

            ld(ones64[:], c_ones64[:, :], "ones64")
            ld(diag[:], c_diag[:, :], "diag")
            ld(gq_b[:], q_norm_g[0:1, :].broadcast_to([128, 64]), "gq_b")
            ld(gk_b[:], k_norm_g[0:1, :].broadcast_to([128, 64]), "gk_b")
            for i in range(2):
                ld(gqT[64 * i:64 * i + 64, :], q_norm_g[0].rearrange("(d o) -> d o", o=1), ("gqT", i), allow_slow_non_contiguous=True)
                ld(gkT[64 * i:64 * i + 64, :], k_norm_g[0].rearrange("(d o) -> d o", o=1), ("gkT", i), allow_slow_non_contiguous=True)
            for i, lv in enumerate((lam_q1, lam_k1, lam_q2, lam_k2)):
                ld(lam4[:, i, :], lv[0:1, :].broadcast_to([128, 64]), ("lam4", i))
            ld(sublng[:], subln_g[0:1, :].broadcast_to([128, 128]), "sublng")
            for h in range(H):
                S.add("dve", lambda e, h=h: e.tensor_scalar(out=gqs[:, h:h + 1], in0=gqT[:], scalar1=float(2.0 ** (h - 2)), scalar2=None, op0=ALU.mult),
                      reads=[("gqT", 0), ("gqT", 1)], writes=[("gqs", h)])
            S.add("dve", lambda e: e.tensor_tensor(out=lamt[:, 0, :], in0=lam4[:, 0, :], in1=lam4[:, 1, :], op=ALU.mult),
                  reads=[("lam4", 0), ("lam4", 1)], writes=[("lamt", 0)])
            S.add("dve", lambda e: e.tensor_tensor(out=lamt[:, 1, :], in0=lam4[:, 2, :], in1=lam4[:, 3, :], op=ALU.mult),
                  reads=[("lam4", 2), ("lam4", 3)], writes=[("lamt", 1)])
            S.add("dve", lambda e: e.reduce_sum(out=lams[:], in_=lamt[:], axis=mybir.AxisListType.X),
                  reads=[("lamt", 0), ("lamt", 1)], writes=["lams"])
            S.add("act", lambda e: e.activation(out=lams[:], in_=lams[:], func=AF.Exp), reads=["lams"], writes=["lams"])
            S.add("dve", lambda e: e.tensor_tensor(out=neglam[:], in0=lams[:, 1:2], in1=lams[:, 0:1], op=ALU.subtract),
                  reads=["lams"], writes=["neglam"])
            S.add("dve", lambda e: e.tensor_scalar(out=neglam[:], in0=neglam[:], scalar1=-float(LAM_INIT), scalar2=None, op0=ALU.add),
                  reads=["neglam"], writes=["neglam"])
            S.add("dve", lambda e: e.reduce_max(out=cmax[:, 0:1], in_=gq_b[:], axis=mybir.AxisListType.X, apply_absolute_value=True),
                  reads=["gq_b"], writes=[("cmax", 0)])
            S.add("dve", lambda e: e.reduce_max(out=cmax[:, 1:2], in_=gk_b[:], axis=mybir.AxisListType.X, apply_absolute_value=True),
                  reads=["gk_b"], writes=[("cmax", 1)])
            S.add("dve", lambda e: e.scalar_tensor_tensor(out=negc[:], in0=cmax[:, 0:1], scalar=-8.0, in1=cmax[:, 1:2], op0=ALU.mult, op1=ALU.mult),
                  reads=[("cmax", 0), ("cmax", 1)], writes=["negc"])
            S.add("dve", lambda e: e.tensor_scalar(out=sublng[:], in0=sublng[:], scalar1=float(1.0 - LAM_INIT), scalar2=None, op0=ALU.mult),
                  reads=["sublng"], writes=["sublng"])

            for (tl, nm) in ((kT, "kaug"), (qTL, "qaugL"), (qTR, "qaugR")):
                S.add("pool", lambda e, tl=tl: e.memset(tl[64:96, 0, :], 0.0), writes=[(nm, 0)])
                S.add("pool", lambda e, tl=tl: e.memset(tl[0:64, 1, :], 0.0), writes=[(nm, 1)])
            for c in range(2):
                r0 = 64 if c == 0 else 0
                ld(kT[r0:r0 + 4, c, :], c_kaug[:, :], ("kaug", c))
                ld(qTL[r0:r0 + 4, c, :], c_qaugL[:, :], ("qaugL", c))
                ld(qTR[r0:r0 + 4, c, :], c_qaugR[:, :], ("qaugR", c))
            for i in (1, 2):
                S.add("pool", lambda e, i=i: e.memset(Vhb[i][:, :, 128:130], 1.0), writes=[("Vones", i)])

            def fill_heads(h):
                return [j for j in (h + 1, h + 2) if j < H] if h % 2 == 0 else []

            def load_wvp(h):
                hs = fill_heads(h)
                if hs:
                    n = 128 * len(hs)
                    ld(Wvp[:, :, 0:n], w_in_v[:, :, 3072 + hs[0] * 128:3072 + hs[0] * 128 + n], "Wvp", eng="pool")

            def vfill_items(h):
                hs = fill_heads(h)
                nh = len(hs)
                items = []
                if nh == 0:
                    return items
                for g in range(16):
                    for q in range(2):
                        kb = 2 * g + q
                        reg = bank(7)[:, q * 128 * nh:(q + 1) * 128 * nh]
                        for kc in range(8):
                            def mm(kb=kb, kc=kc, reg=reg, q=q, g=g):
                                S.add("pe", lambda e: e.matmul(reg, lhsT=xnT[:, kc, kb * 128:(kb + 1) * 128], rhs=Wvp[:, kc, 0:128 * nh], start=(kc == 0), stop=(kc == 7), skip_group_check=True),
                                      reads=[("xnT", kb), "Wvp"], writes=[bk(7)])
                                if q == 1 and kc == 7:
                                    src = bank(7)[:, 0:256 * nh].rearrange("p (q j n) -> p q j n", q=2, j=nh)
                                    for j, hj in enumerate(hs):
                                        S.add("dve", lambda e, j=j, hj=hj: e.tensor_copy(out=Vhb[hj % 3][:, 2 * g:2 * g + 2, 0:128], in_=src[:, :, j, :]),
                                              reads=[bk(7)], writes=[("Vh", hj % 3, 2 * g), ("Vh", hj % 3, 2 * g + 1)])
                            items.append(mm)
                return items

            load_wvp(0)

            accb = [4, 5, 6]

            def acc(c, qi):
                idx = c * 4 + qi
                return bank(accb[idx // 3])[:, (idx % 3) * 160:(idx % 3) * 160 + 129]

            def acck(c, qi):
                return bk(accb[(c * 4 + qi) // 3])


            def do_head(h):
                wb = h % 2
                Vh = Vhb[h % 3]
                vfill = vfill_items(h)
                if h % 2 == 1 and h + 1 < H:
                    load_wvp(h + 1)
                groups = [(0, tg) for tg in range(8)] + [(1, tg) for tg in range(4)]

                def proj_pe(s_):
                    isq, tg = groups[s_]
                    pK = bank(s_ % 4)
                    col0 = 128 if isq == 0 else 0
                    wkey = ("wk", wb) if isq == 0 else ("wq", wb)
                    for kc in range(8):
                        S.add("pe", lambda e, kc=kc: e.matmul(pK, lhsT=wqk[wb][:, kc, col0:col0 + 128], rhs=xnT[:, kc, tg * 512:(tg + 1) * 512], start=(kc == 0), stop=(kc == 7)),
                              reads=[wkey] + [("xnT", 4 * tg + i) for i in range(4)], writes=[bk(s_ % 4)])
                    S.add("act", lambda e: e.activation(out=sq[s_ % 2][:], in_=pK, func=AF.Square),
                          reads=[bk(s_ % 4)], writes=[("sq", s_ % 2)])

                def proj_post(s_):
                    isq, tg = groups[s_]
                    pK = bank(s_ % 4)
                    pSS = bank(4 + s_ % 2)
                    pb = s_ % 2
                    tcols = slice(tg * 512, (tg + 1) * 512)
                    S.add("pe", lambda e: e.matmul(pSS, lhsT=ones64[:], rhs=sq[pb][:], start=True, stop=True),
                          reads=["ones64", ("sq", pb)], writes=[bk(4 + pb)])
                    S.add("act", lambda e: e.activation(out=rk[pb][:], in_=pSS, func=AF.Ln, bias=epsb[:, 0:1], scale=1.0),
                          reads=[bk(4 + pb), "epsb"], writes=[("rk", pb)])
                    S.add("act", lambda e: e.activation(out=rk[pb][:], in_=rk[pb][:], func=AF.Exp, scale=-0.5),
                          reads=[("rk", pb)], writes=[("rk", pb)])
                    for c in range(2):
                        rows = slice(64 * c, 64 * c + 64)
                        if isq == 0:
                            S.add("dve", lambda e, c=c, rows=rows: e.scalar_tensor_tensor(out=kT[rows, c, tcols], in0=pK[rows, :], scalar=gkT[rows, 0:1], in1=rk[pb][rows, :], op0=ALU.mult, op1=ALU.mult),
                                  reads=[bk(s_ % 4), ("rk", pb), ("gkT", c)], writes=[("kT", c, tg)])
                        else:
                            S.add("dve", lambda e, c=c, rows=rows: e.scalar_tensor_tensor(out=qTL[rows, c, tcols], in0=pK[rows, :], scalar=gqs[rows, h:h + 1], in1=rk[pb][rows, :], op0=ALU.mult, op1=ALU.mult),
                                  reads=[bk(s_ % 4), ("rk", pb), ("gqs", h)], writes=[("qTL", c, tg)])
                            S.add("pool", lambda e, c=c, rows=rows: e.tensor_copy(out=qTR[rows, c, tcols], in_=qTL[rows, c, tcols]),
                                  reads=[("qTL", c, tg)], writes=[("qTR", c, tg)])

                for s_ in range(len(groups) + 1):
                    if s_ < len(groups):
                        proj_pe(s_)
                    if s_ >= 1:
                        proj_post(s_ - 1)
                if h + 1 < H:
                    load_wqk(h + 1)

                slope = float(2.0 ** (-(h + 1)))
                iters = [(G, kb) for G in range(4) for kb in range(32)]

                def emit_qk(it):
                    G, kb = iters[it]
                    sb = it % 2
                    pS = PSP[sb]
                    late = []
                    for c in range(2):
                        kkey = ("kT", c, kb // 4)
                        kaugk = ("kaug", c)
                        kcols = slice(kb * 128, (kb + 1) * 128)

                        rall = slice(0, 96) if c == 0 else slice(0, 128)
                        rdat = slice(0, 64) if c == 0 else slice(64, 128)

                        def mmL(q0, q1, c=c, kcols=kcols, rall=rall):
                            return lambda e: e.matmul(pS[:, c, q0 * 128:q1 * 128], lhsT=kT[rall, c, kcols], rhs=qTL[rall, c, (4 * G + q0) * 128:(4 * G + q1) * 128], start=True, stop=True)

                        def mmR(q0, q1, c=c, kcols=kcols, rall=rall):
                            return lambda e: e.matmul(pS[:, c, q0 * 128:q1 * 128], lhsT=kT[rall, c, kcols], rhs=qTR[rall, c, (4 * G + q0) * 128:(4 * G + q1) * 128], start=True, stop=True)

                        rdL = [kkey, kaugk, ("qTL", c, G), ("qaugL", c)]
                        rdR = [kkey, kaugk, ("qTR", c, G), ("qaugR", c)]
                        wr = [bk(2 * sb + c)]
                        if kb >= 16 or kb < 4 * G:
                            S.add("pe", mmL(0, 4), reads=rdL, writes=wr)
                        elif kb > 4 * G + 3:
                            S.add("pe", mmR(0, 4), reads=rdR, writes=wr)
                        else:
                            d = kb - 4 * G
                            if d > 0:
                                S.add("pe", mmR(0, d), reads=rdR, writes=wr)
                            S.add("pe", lambda e, c=c, kcols=kcols, d=d, rall=rall: e.matmul(pS[:, c, d * 128:512], lhsT=kT[rall, c, kcols], rhs=qTL[rall, c, (4 * G + d) * 128:(4 * G + 4) * 128], start=True, stop=False, skip_group_check=True),
                                  reads=rdL, writes=wr)
                            late.append((lambda e, c=c, d=d: e.matmul(pS[:, c, d * 128:(d + 1) * 128], lhsT=ident[:], rhs=diag[:], start=False, stop=True, skip_group_check=True), wr))

                    for fn_, wr_ in late:
                        S.add("pe", fn_, reads=["ident", "diag"], writes=wr_)

                def emit_exp_av(it):
                    G, kb = iters[it]
                    sb = it % 2
                    pt = it % 3
                    pS = PSP[sb]
                    S.add("act", lambda e: e.activation(out=PT[pt][:], in_=pS[:], func=AF.Exp, bias=negc[:, 0:1], scale=slope),
                          reads=[bk(2 * sb), bk(2 * sb + 1), "negc"], writes=[("PT", pt)])
                    for c in range(2):
                        for qi in range(4):
                            S.add("pe", lambda e, c=c, qi=qi: e.matmul(acc(c, qi), lhsT=PT[pt][:, c, qi * 128:(qi + 1) * 128], rhs=Vh[:, kb, 0:129], start=(kb == 0 and (c * 4 + qi) % 3 == 0), stop=(kb == 31), skip_group_check=True),
                                  reads=[("PT", pt), ("Vh", h % 3, kb), ("Vones", h % 3)], writes=[acck(c, qi)])

                def emit_epilogue(G):
                    for b3 in range(3):
                        ns = 3 if b3 < 2 else 2
                        S.add("dve", lambda e, b3=b3, ns=ns: e.tensor_copy(out=accs[:, 3 * b3:3 * b3 + ns, 0:129], in_=bank(accb[b3])[:, 0:160 * ns].rearrange("p (s d) -> p s d", s=ns)[:, :, 0:129]),
                              reads=[bk(accb[b3])], writes=[("accs", b3)])
                    ak = [("accs", 0), ("accs", 1), ("accs", 2)]
                    S.add("dve", lambda e: e.reciprocal(out=rec9[:].unsqueeze(2), in_=accs[:, 0:8, 128:129]), reads=ak, writes=["rec9"])
                    S.add("dve", lambda e: e.tensor_scalar(out=rec9[:, 4:8], in0=rec9[:, 4:8], scalar1=neglam[:, 0:1], scalar2=None, op0=ALU.mult),
                          reads=["rec9", "neglam"], writes=["rec9"])
                    S.add("dve", lambda e: e.tensor_tensor(out=accs[:, 0:8, 0:128], in0=accs[:, 0:8, 0:128], in1=rec9[:].unsqueeze(2).broadcast_to([128, 8, 128]), op=ALU.mult),
                          reads=ak + ["rec9"], writes=ak)
                    S.add("dve", lambda e: e.tensor_tensor(out=o4[:], in0=accs[:, 0:4, 0:128], in1=accs[:, 4:8, 0:128], op=ALU.add), reads=ak, writes=["o4"])
                    S.add("dve", lambda e: e.tensor_tensor(out=accs[:, 0:4, 0:128], in0=o4[:], in1=o4[:], op=ALU.mult), reads=["o4"], writes=ak)
                    S.add("dve", lambda e: e.reduce_sum(out=ssall[:, 4 * G:4 * G + 4, h], in_=accs[:, 0:4, 0:128], axis=mybir.AxisListType.X),
                          reads=ak, writes=[("ssall", h, G)])
                    S.add("pool", lambda e: e.tensor_copy(out=oda[:, 4 * G:4 * G + 4, h * 128:(h + 1) * 128], in_=o4[:]),
                          reads=["o4"], writes=[("B32", 4 * G + q) for q in range(4)])

                emit_qk(0)
                for it in range(len(iters)):
                    if it + 1 < len(iters):
                        emit_qk(it + 1)
                    for _ in range(2):
                        if vfill:
                            vfill.pop(0)()
                    emit_exp_av(it)
                    if iters[it][1] == 31:
                        emit_epilogue(iters[it][0])

            for h_ in range(H):
                do_head(h_)

            allss = [("ssall", h, G) for h in range(H) for G in range(4)]
            allb = [("B32", t) for t in range(NT_OWN)]
            S.add("act", lambda e: e.activation(out=ssall[:], in_=ssall[:], func=AF.Sqrt, bias=epsb[:, 0:1], scale=1.0 / 128),
                  reads=allss + ["epsb"], writes=allss)
            S.add("dve", lambda e: e.reciprocal(out=ssall[:], in_=ssall[:]), reads=allss, writes=allss + ["ssall_r"])
            if dbg:
                for t in range(NT_OWN):
                    S.add("sp", lambda e, t=t: e.dma_start(out=ODA[t * 128:(t + 1) * 128, :], in_=oda[:, t, :]), reads=[("B32", t)], dma_sem=("oda_dbg", t), final=True)
            S.barrier()

        if "C" in phases:
            PB = wa(0, 8192).rearrange("p (k n) -> p k n", k=8)
            Wout = wa(8192, 8192).rearrange("p (k n) -> p k n", k=8)
            RB = [18432, 0]

            def ffn_views(i):
                base = RB[i % 2]
                nf = FBLOCKS[i][1]
                wg = wa(base, 6144).rearrange("p (k n) -> p k n", k=8)
                wu = wa(base + 6144, 6144).rearrange("p (k n) -> p k n", k=8)
                wd = wa(base + 12288, 6144).rearrange("p (f n) -> p f n", f=6)
                return wg, wu, wd, nf

            def rpart(i, part):
                return ("Rp", 1 if RB[i % 2] else 0, part)

            aT = TAb[:, 2048:5120].rearrange("p (f n) -> p f n", f=6)
            ld(PB, w_proj_da.rearrange("(kc p) n -> p kc n", p=128), "PB", eng="pool")
            ld(Wout, w_out.rearrange("(kc p) n -> p kc n", p=128), "Wout", eng="pool")

            def load_ffn(i):
                wg, wu, wd, nf = ffn_views(i)
                f0 = FBLOCKS[i][0]
                extra = ["PB", "Wout"] if RB[i % 2] == 0 else []
                r = 1 if RB[i % 2] else 0
                S.add("pool", lambda e: e.dma_start(out=wg[:, :, 0:nf * 128], in_=w_ffn_gate.rearrange("(kc p) n -> p kc n", p=128)[:, :, f0 * 128:(f0 + nf) * 128]),
                      writes=[rpart(i, 0)] + extra, dma_sem=("ffn", r, 0))
                S.add("pool", lambda e: e.dma_start(out=wu[:, :, 0:nf * 128], in_=w_ffn_up.rearrange("(kc p) n -> p kc n", p=128)[:, :, f0 * 128:(f0 + nf) * 128]),
                      writes=[rpart(i, 1)] + extra, dma_sem=("ffn", r, 1))
                S.add("pool", lambda e: e.dma_start(out=wd[:, 0:nf, :], in_=w_ffn_down.rearrange("(f p) n -> p f n", p=128)[:, f0:f0 + nf, :]),
                      writes=[rpart(i, 2)] + extra, dma_sem=("ffn", r, 2))

            load_ffn(0)
            zf = TA[:, 0:1024]
            odaT = TAb[:, 2048:3072].rearrange("p (k t) -> p k t", k=8)
            zl = [TAb[:, 3072:4096]] * 2
            gl = [TAb[:, 4096:5120]] * 2
            zb = TAb[:, 5120:6144]
            zT = TAb[:, 6144:7168].rearrange("p (k t) -> p k t", k=8)
            sig = [TA[:, i * 512:(i + 1) * 512] for i in range(2)]

            def c_a0(t):
                ld(hacc[:, t, :], x[t * 128:(t + 1) * 128, :], ("hacc", t))
                odat = oda[:, t, :].rearrange("p (h d) -> p h d", h=8)
                S.add("dve", lambda e: e.tensor_tensor(out=odat, in0=odat, in1=ssall[:, t, :].unsqueeze(2).broadcast_to([128, 8, 128]), op=ALU.mult),
                      reads=[("B32", t), "ssall_r"], writes=[("B32", t)])
                S.add("dve", lambda e: e.tensor_tensor(out=odat, in0=odat, in1=sublng[:].unsqueeze(1).broadcast_to([128, 8, 128]), op=ALU.mult),
                      reads=[("B32", t), "sublng"], writes=[("B32", t)])

            def c_a1(t):
                ld(zl[0], ZSG[t * 128:(t + 1) * 128, :], ("zl", 0))
                ld(gl[0], GDA[t * 128:(t + 1) * 128, :], ("gl", 0))
                for hh in range(8):
                    S.add("pe", lambda e, hh=hh: e.matmul(bank(4 + hh // 4)[:, (hh % 4) * 128:(hh % 4 + 1) * 128], lhsT=oda[:, t, hh * 128:(hh + 1) * 128], rhs=ident[:], start=True, stop=True),
                          reads=[("B32", t), "ident"], writes=[bk(4 + hh // 4)])
                S.add("act", lambda e: e.copy(out=odaT[:], in_=PSP[2][:].rearrange("p a (b t) -> p (a b) t", t=128)), reads=[bk(4), bk(5)], writes=["odaT"])

            def c_a2(t):
                for j in range(2):
                    for hh in range(8):
                        S.add("pe", lambda e, hh=hh, j=j: e.matmul(bank(2 + j), lhsT=odaT[:, hh, :], rhs=PB[:, hh, j * 512:(j + 1) * 512], start=(hh == 0), stop=(hh == 7)),
                              reads=["odaT", "PB"], writes=[bk(2 + j)])
                S.add("dve", lambda e: e.tensor_tensor(out=zf[:], in0=PSP[1][:].rearrange("p a n -> p (a n)"), in1=gl[0], op=ALU.mult),
                      reads=[bk(2), bk(3), ("gl", 0)], writes=["zf"])
                S.add("dve", lambda e: e.tensor_tensor(out=zb[:], in0=zf[:], in1=zl[0], op=ALU.add), reads=["zf", ("zl", 0)], writes=["zb"])

            def c_a3(t):
                for kc in range(8):
                    S.add("pe", lambda e, kc=kc: e.matmul(bank(kc // 4)[:, (kc % 4) * 128:(kc % 4 + 1) * 128], lhsT=zb[:, kc * 128:(kc + 1) * 128], rhs=ident[:], start=True, stop=True),
                          reads=["zb", "ident"], writes=[bk(kc // 4)])
                S.add("act", lambda e: e.copy(out=zT[:], in_=PSP[0][:].rearrange("p a (b t) -> p (a b) t", t=128)), reads=[bk(0), bk(1)], writes=["zT"])

            def c_a4(t):
                for j in range(2):
                    for kc in range(8):
                        S.add("pe", lambda e, kc=kc, j=j: e.matmul(bank(6 + j), lhsT=zT[:, kc, :], rhs=Wout[:, kc, j * 512:(j + 1) * 512], start=(kc == 0), stop=(kc == 7)),
                              reads=["zT", "Wout"], writes=[bk(6 + j)])
                S.add("dve", lambda e: e.tensor_tensor(out=hacc[:, t, :], in0=PSP[3][:].rearrange("p a n -> p (a n)"), in1=hacc[:, t, :], op=ALU.add),
                      reads=[bk(6), bk(7), ("hacc", t)], writes=[("hacc", t)])

            def c_b1(t):
                rms_stage1(hacc[:, t, :], ("hacc", t), t % 2)

            def c_b2(t):
                rms_stage2(g2T, "g2T", hnT[:, t, :, :], ("B32", t), t % 2)

            stages = [c_a0, c_a1, c_a2, c_a3, c_a4, c_b1, c_b2]
            for s_ in range(NT_OWN + len(stages) - 1):
                for k in reversed(range(len(stages))):
                    t = s_ - k
                    if 0 <= t < NT_OWN:
                        stages[k](t)

            for i in range(len(FBLOCKS)):
                wg, wu, wd, nf = ffn_views(i)
                if i + 1 < len(FBLOCKS):
                    load_ffn(i + 1)
                for G in range(4):
                    for f in range(nf):
                        pb = f % 2
                        for (wmat, wkey, bi) in ((wg, rpart(i, 0), 0), (wu, rpart(i, 1), 1)):
                            for kc in range(8):
                                S.add("pe", lambda e, kc=kc, f=f, wmat=wmat, pb=pb, bi=bi, G=G: e.matmul(bank(2 * pb + bi), lhsT=wmat[:, kc, f * 128:(f + 1) * 128], rhs=hnT[:, 4 * G:4 * G + 4, kc, :], start=(kc == 0), stop=(kc == 7)),
                                      reads=[wkey] + [("B32", 4 * G + q) for q in range(4)], writes=[bk(2 * pb + bi)])
                        S.add("act", lambda e, pb=pb: e.activation(out=sig[pb][:], in_=bank(2 * pb), func=AF.Sigmoid), reads=[bk(2 * pb)], writes=[("sig", pb), "zf"])
                        S.add("dve", lambda e, pb=pb: e.tensor_tensor(out=sig[pb][:], in0=bank(2 * pb), in1=sig[pb][:], op=ALU.mult),
                              reads=[bk(2 * pb), ("sig", pb)], writes=[("sig", pb)])
                        S.add("dve", lambda e, pb=pb, f=f: e.tensor_tensor(out=aT[:, f, :], in0=bank(2 * pb + 1), in1=sig[pb][:], op=ALU.mult),
                              reads=[bk(2 * pb + 1), ("sig", pb)], writes=[("aT", f), "odaT", ("zl", 0), ("gl", 0)])
                    for ti in range(4):
                        t = 4 * G + ti
                        db = ti % 2
                        for j in range(2):
                            for f in range(nf):
                                S.add("pe", lambda e, f=f, j=j, ti=ti, db=db, wd=wd, nf=nf: e.matmul(bank(4 + 2 * db + j), lhsT=aT[:, f, ti * 128:(ti + 1) * 128], rhs=wd[:, f, j * 512:(j + 1) * 512], start=(f == 0), stop=(f == nf - 1)),
                                      reads=[("aT", f), rpart(i, 2)], writes=[bk(4 + 2 * db + j)])
                        S.add("dve", lambda e, t=t, db=db: e.tensor_tensor(out=hacc[:, t, :], in0=PSP[2 + db][:].rearrange("p a n -> p (a n)"), in1=hacc[:, t, :], op=ALU.add),
                              reads=[bk(4 + 2 * db), bk(5 + 2 * db), ("hacc", t)], writes=[("hacc", t)])
                        if i == len(FBLOCKS) - 1:
                            S.add("sp", lambda e, t=t: e.dma_start(out=out[t * 128:(t + 1) * 128, :], in_=hacc[:, t, :]),
                                  reads=[("hacc", t)], dma_sem=("out", t), final=True)

        if dbg and "A" in phases:
            pass
        nsem = S.emit(nc, st)
    return nc


def _bf16(a):
    return np.asarray(a, dtype=np.float32).astype(ml_dtypes.bfloat16)


def make_consts(half):
    ident = _bf16(np.eye(128))
    ones64 = np.zeros((128, 128), np.float32)
    ones64[:64, :64] = 1.0 / 64
    ones64[64:, 64:] = 1.0 / 64
    ones64 = _bf16(ones64)
    onesrow = _bf16(np.ones((1, 512)))
    ii = np.arange(128)
    diag = _bf16(-2.0 * np.maximum(ii[:, None] - ii[None, :], 0))
    own_blocks = half * 16 + np.arange(16)
    oth_blocks = (1 - half) * 16 + np.arange(16)
    kblocks = np.concatenate([own_blocks, oth_blocks])
    sign = np.concatenate([np.ones(16), np.full(16, 1.0 if half == 1 else -1.0)])
    kaug = np.zeros((4, S_ALL), np.float32)
    for kb in range(32):
        sl = slice(kb * 128, (kb + 1) * 128)
        kaug[0, sl] = sign[kb] * 128.0 * kblocks[kb]
        kaug[1, sl] = sign[kb] * ii
        kaug[2, sl] = sign[kb]
        kaug[3, sl] = sign[kb]
    qL = np.zeros((4, S_OWN), np.float32)
    for qb in range(16):
        sl = slice(qb * 128, (qb + 1) * 128)
        qL[0, sl] = 1.0
        qL[1, sl] = 1.0
        qL[2, sl] = -128.0 * own_blocks[qb]
        qL[3, sl] = -ii
    e8 = np.zeros((8, 512), np.float32)
    for g in range(8):
        e8[g, g * 64:(g + 1) * 64] = 1.0
    return {"c_ident": ident, "c_ones64": ones64, "c_onesrow": onesrow, "c_diag": diag, "c_e8": _bf16(e8),
            "c_kaug": _bf16(kaug), "c_qaugL": _bf16(qL), "c_qaugR": _bf16(-qL)}


def make_in_maps(inputs):
    f = lambda a: np.ascontiguousarray(np.asarray(a, dtype=np.float32))
    x = f(inputs["x"])
    shared = {
        "norm1_g": f(inputs["norm1_g"]).reshape(1, D),
        "w_in": f(inputs["w_in"])[0],
        "b_gate": f(inputs["b_gate"]).reshape(1, 2048),
        "sg_ln_g": f(inputs["sg_ln_g"]).reshape(1, 512),
        "sg_ln_b": f(inputs["sg_ln_b"]).reshape(1, 512),
        "sg_wT": np.ascontiguousarray(f(inputs["sg_w"])[0].transpose(2, 0, 1)),
        "sg_b": f(inputs["sg_b"]).reshape(8, 128),
        "q_norm_g": f(inputs["q_norm_g"]).reshape(1, 64),
        "k_norm_g": f(inputs["k_norm_g"]).reshape(1, 64),
        "lam_q1": f(inputs["lam_q1"]).reshape(1, 64),
        "lam_k1": f(inputs["lam_k1"]).reshape(1, 64),
        "lam_q2": f(inputs["lam_q2"]).reshape(1, 64),
        "lam_k2": f(inputs["lam_k2"]).reshape(1, 64),
        "subln_g": f(inputs["subln_g"]).reshape(1, 128),
        "w_proj_sg": f(inputs["w_proj_sg"])[0],
        "w_proj_da": f(inputs["w_proj_da"])[0],
        "w_out": f(inputs["w_out"])[0],
        "norm2_g": f(inputs["norm2_g"]).reshape(1, D),
        "w_ffn_gate": f(inputs["w_ffn_gate"])[0],
        "w_ffn_up": f(inputs["w_ffn_up"])[0],
        "w_ffn_down": f(inputs["w_ffn_down"])[0],
    }
    consts = [make_consts(0), make_consts(1)]
    in_maps = []
    for c in range(8):
        b, half = divmod(c, 2)
        own = x[b, half * S_OWN:(half + 1) * S_OWN]
        oth = x[b, (1 - half) * S_OWN:(2 - half) * S_OWN]
        m = dict(shared)
        m["x"] = np.ascontiguousarray(np.concatenate([own, oth], axis=0))
        m.update(consts[half])
        in_maps.append(m)
    return in_maps


def kernel(**inputs):
    nc = build_program()
    in_maps = make_in_maps(inputs)
    res = run_bass_kernel_spmd(nc, in_maps, core_ids=list(range(8)))
    out = np.empty((4, S_ALL, D), np.float32)
    for c in range(8):
        b, half = divmod(c, 2)
        out[b, half * S_OWN:(half + 1) * S_OWN] = np.asarray(res.results[c]["out"], dtype=np.float32)
    return out
```

```python
import os
import math
import numpy as np
import ml_dtypes
from contextlib import ExitStack
import concourse.bass as bass
import concourse.mybir as mybir
from concourse.bass_utils import run_bass_kernel_spmd

F32 = mybir.dt.float32
BF16 = mybir.dt.bfloat16
AF = mybir.ActivationFunctionType
ALU = mybir.AluOpType

EPS = 1e-6
D = 1024
S_OWN = 2048
S_ALL = 4096
NT_OWN = 16
NT_ALL = 32
H = 8
DFF = 2816
LAM_INIT = 0.8 - 0.6 * math.exp(-0.3 * 0)
FBLOCKS = [(0, 6), (6, 6), (12, 5), (17, 5)]


class Sched:
    ENGS = ("pe", "act", "dve", "pool", "sp")

    def __init__(self):
        self.q = {e: [] for e in self.ENGS}
        self.ncomp = {e: 0 for e in self.ENGS}
        self.lastw = {}
        self.readers = {}
        self.dma_cnt = {}
        self.final_tokens = []
        self.pending = {e: None for e in self.ENGS}

    def barrier(self):
        toks = {}
        for e in self.ENGS:
            if self.ncomp[e] > 0:
                toks[("eng", e)] = self.ncomp[e]
        for k, v in self.dma_cnt.items():
            toks[("dma", k)] = v
        for e in self.ENGS:
            self.pending[e] = dict(toks)

    def add(self, eng, fn, reads=(), writes=(), dma_sem=None, final=False):
        waits = {}

        def need(tok):
            if tok is None:
                return
            s, v, teng = tok
            if teng == eng and eng == "pe" and dma_sem is None:
                return
            if waits.get(s, 0) < v:
                waits[s] = v

        if self.pending[eng]:
            for s, v in self.pending[eng].items():
                if s == ("eng", "pe") and eng == "pe":
                    continue
                waits[s] = max(waits.get(s, 0), v)
            self.pending[eng] = None
        for k in reads:
            need(self.lastw.get(k))
        for k in writes:
            need(self.lastw.get(k))
            for s, (v, teng) in self.readers.get(k, {}).items():
                need((s, v, teng))
        if dma_sem is None:
            self.ncomp[eng] += 1
            tok = (("eng", eng), self.ncomp[eng], eng)
        else:
            self.dma_cnt[dma_sem] = self.dma_cnt.get(dma_sem, 0) + 16
            tok = (("dma", dma_sem), self.dma_cnt[dma_sem], None)
        for k in reads:
            d = self.readers.setdefault(k, {})
            if d.get(tok[0], (0, None))[0] < tok[1]:
                d[tok[0]] = (tok[1], tok[2])
        for k in writes:
            self.lastw[k] = tok
            self.readers[k] = {}
        self.q[eng].append((fn, waits, tok))
        if final:
            self.final_tokens.append(tok)
        return tok

    def emit(self, nc, stack):
        sems = {}

        def sem(key):
            if key not in sems:
                sems[key] = stack.enter_context(nc.semaphore("s%d" % len(sems)))
            return sems[key]

        for e in self.ENGS:
            for fn, waits, tok in self.q[e]:
                sem(tok[0])
        engobj = {"pe": "tensor", "act": "scalar", "dve": "vector", "pool": "gpsimd", "sp": "sync"}
        finals = list(self.final_tokens)
        with nc.Block() as block:
            for e in self.ENGS:
                ops = self.q[e]
                if not ops:
                    continue

                def body(eng, ops=ops, e=e):
                    waited = {}
                    for fn, waits, tok in ops:
                        pend = []
                        for s, v in waits.items():
                            if waited.get(s, 0) >= v:
                                continue
                            waited[s] = v
                            pend.append((s, v))
                        attach = None
                        if pend and tok[0][0] != "dma":
                            attach = pend.pop()
                        for s, v in pend:
                            eng.wait_ge(sem(s), v)
                        ins = fn(eng)
                        if attach is not None:
                            ins._wait_ge(sem(attach[0]), attach[1])
                        ins.then_inc(sem(tok[0]), 16 if tok[0][0] == "dma" else 1)
                    if e == "sp":
                        for tok in finals:
                            if waited.get(tok[0], 0) < tok[1]:
                                waited[tok[0]] = tok[1]
                                eng.wait_ge(sem(tok[0]), tok[1])

                getattr(block, engobj[e])(body)
        return len(sems)


def build_program(phases="ABC", dbg=False):
    nc = bass.Bass("TRN2", target_bir_lowering=False)

    def din(name, shape, dt=F32):
        return nc.dram_tensor(name, shape, dt, kind="ExternalInput").ap()

    x = din("x", [S_ALL, D])
    norm1_g = din("norm1_g", [1, D])
    w_in = din("w_in", [D, 6144])
    b_gate = din("b_gate", [1, 2048])
    sg_ln_g = din("sg_ln_g", [1, 512])
    sg_ln_b = din("sg_ln_b", [1, 512])
    sg_wT = din("sg_wT", [128, 8, 128])
    sg_b = din("sg_b", [8, 128])
    q_norm_g = din("q_norm_g", [1, 64])
    k_norm_g = din("k_norm_g", [1, 64])
    lam_q1 = din("lam_q1", [1, 64])
    lam_k1 = din("lam_k1", [1, 64])
    lam_q2 = din("lam_q2", [1, 64])
    lam_k2 = din("lam_k2", [1, 64])
    subln_g = din("subln_g", [1, 128])
    w_proj_sg = din("w_proj_sg", [512, D])
    w_proj_da = din("w_proj_da", [D, D])
    w_out = din("w_out", [D, D])
    norm2_g = din("norm2_g", [1, D])
    w_ffn_gate = din("w_ffn_gate", [D, DFF])
    w_ffn_up = din("w_ffn_up", [D, DFF])
    w_ffn_down = din("w_ffn_down", [DFF, D])
    c_ident = din("c_ident", [128, 128], BF16)
    c_ones64 = din("c_ones64", [128, 128], BF16)
    c_onesrow = din("c_onesrow", [1, 512], BF16)
    c_e8 = din("c_e8", [8, 512], BF16)
    c_diag = din("c_diag", [128, 128], BF16)
    c_kaug = din("c_kaug", [4, S_ALL], BF16)
    c_qaugL = din("c_qaugL", [4, S_OWN], BF16)
    c_qaugR = din("c_qaugR", [4, S_OWN], BF16)

    out = nc.dram_tensor("out", [S_OWN, D], F32, kind="ExternalOutput").ap()
    skind = "ExternalOutput" if dbg else "Internal"
    ZSG = nc.dram_tensor("zsg_scr", [S_OWN, D], BF16, kind=skind).ap()
    GDA = nc.dram_tensor("gda_scr", [S_OWN, D], BF16, kind=skind).ap()
    ODA = nc.dram_tensor("oda_dbg", [S_OWN, D], BF16, kind=skind).ap() if dbg else None

    w_in_v = w_in.rearrange("(kc p) n -> p kc n", p=128)

    S = Sched()
    with ExitStack() as st:
        def sbt(name, shape, dt):
            return st.enter_context(nc.sbuf_tensor(name, shape, dt))

        A64 = sbt("A64", [128, 32768], BF16)
        WA = sbt("WA", [128, 37888], BF16)
        B32 = sbt("B32", [128, 16384], BF16)
        TA = sbt("TA", [128, 3584], F32)
        B32f = B32[:].bitcast(F32)
        TAb = TA[:].bitcast(BF16)
        xnT = A64[:].rearrange("p (k t) -> p k t", k=8)
        hacc = A64[:].bitcast(F32).rearrange("p (t d) -> p t d", t=16)
        oda = B32[:].rearrange("p (t d) -> p t d", t=16)
        hnT = B32[:].rearrange("p (t k n) -> p t k n", t=16, k=8)

        def wa(off, n):
            return WA[:, off:off + n]

        PSP = [st.enter_context(nc.psum_tensor("psp%d" % i, [128, 2, 512], F32)) for i in range(4)]

        def bank(i):
            return PSP[i // 2][:, i % 2, :]

        def bk(i):
            return ("ps", i)

        ident = sbt("ident", [128, 128], BF16)
        ones64 = sbt("ones64", [128, 128], BF16)
        onesrow = sbt("onesrow", [1, 512], BF16)
        diag = sbt("diag", [128, 128], BF16)
        g1T = sbt("g1T", [128, 8], F32)
        g2T = sbt("g2T", [128, 8], F32)
        lng_b = B32f[:, 1024:1536]
        lnb_b = B32f[:, 1536:2048]
        bs8 = TAb[0:8, 0:128]
        e8 = sbt("e8", [8, 512], BF16)
        bgate_row = TAb[0:1, 1024:3072]
        epsb = sbt("epsb", [128, 1], F32)
        gq_b = sbt("gq_b", [128, 64], F32)
        gk_b = sbt("gk_b", [128, 64], F32)
        gqT = sbt("gqT", [128, 1], F32)
        gkT = sbt("gkT", [128, 1], F32)
        gqs = sbt("gqs", [128, 8], F32)
        lam4 = sbt("lam4", [128, 4, 64], F32)
        lamt = sbt("lamt", [128, 2, 64], F32)
        lams = sbt("lams", [128, 2], F32)
        neglam = sbt("neglam", [128, 1], F32)
        negc = sbt("negc", [128, 1], F32)
        cmax = sbt("cmax", [128, 2], F32)
        sublng = sbt("sublng", [128, 128], F32)

        def ld(dst_ap, src_ap, key, eng="sp", **kw):
            S.add(eng, lambda e: e.dma_start(out=dst_ap, in_=src_ap, **kw), writes=[key], dma_sem=key)

        ld(ident[:], c_ident[:, :], "ident")
        ld(onesrow[:], c_onesrow[:, :], "onesrow")
        ld(g1T[:], norm1_g[0].rearrange("(kc p) -> p kc", p=128), "g1T", allow_slow_non_contiguous=True)
        ld(g2T[:], norm2_g[0].rearrange("(kc p) -> p kc", p=128), "g2T", allow_slow_non_contiguous=True)
        ld(lng_b, sg_ln_g[0:1, :].broadcast_to([128, 512]), "lng_b")
        ld(lnb_b, sg_ln_b[0:1, :].broadcast_to([128, 512]), "lnb_b")
        ld(bs8, sg_b[:, :], "bs8", eng="pool")
        ld(e8[:], c_e8[:, :], "e8")
        ld(bgate_row, b_gate[:, :], "bgate_row", eng="pool")
        S.add("dve", lambda e: e.memset(epsb[:], EPS), writes=["epsb"])
        XT = sbt("XT", [128, 2080], F32)
        xt = [XT[:, 0:1024], XT[:, 1024:2048]]
        ss = [sbt("ss%d" % i, [128, 1], F32) for i in range(2)]
        xs = [sbt("xs%d" % i, [128, D], BF16) for i in range(2)]
        vt = [B32[:, 14336 + i * 1024:14336 + (i + 1) * 1024] for i in range(2)]

        def rms_stage1(src_ap, src_key, b):
            S.add("act", lambda e: e.activation(out=xs[b][:], in_=src_ap, func=AF.Square, scale=1.0 / 32, accum_out=ss[b][:]),
                  reads=[src_key], writes=[("xs", b), ("ss", b)])
            S.add("act", lambda e: e.activation(out=ss[b][:], in_=ss[b][:], func=AF.Sqrt, bias=epsb[:, 0:1], scale=1.0),
                  reads=[("ss", b), "epsb"], writes=[("ss", b)])
            S.add("dve", lambda e: e.reciprocal(out=ss[b][:], in_=ss[b][:]), reads=[("ss", b)], writes=[("ss", b)])
            S.add("dve", lambda e: e.tensor_scalar(out=xs[b][:], in0=src_ap, scalar1=ss[b][:, 0:1], scalar2=None, op0=ALU.mult),
                  reads=[src_key, ("ss", b)], writes=[("xs", b)])

        def rms_stage2(gT, gkey, dst_ap, dst_key, b):
            for kc in range(8):
                S.add("pe", lambda e, kc=kc: e.matmul(bank(kc // 4)[:, (kc % 4) * 128:(kc % 4 + 1) * 128], lhsT=xs[b][:, kc * 128:(kc + 1) * 128], rhs=ident[:], start=True, stop=True),
                      reads=[("xs", b), "ident"], writes=[bk(kc // 4)])
            S.add("dve", lambda e: e.tensor_tensor(out=dst_ap, in0=PSP[0][:].rearrange("p a (b t) -> p (a b) t", t=128),
                                                   in1=gT[:].unsqueeze(2).broadcast_to([128, 8, 128]), op=ALU.mult),
                  reads=[bk(0), bk(1), gkey], writes=[dst_key])

        wqk = [wa(24576, 2048).rearrange("p (k n) -> p k n", k=8), wa(20544, 2048).rearrange("p (k n) -> p k n", k=8)]
        Wvh = [wa(26624, 1024).rearrange("p (k n) -> p k n", k=8), wa(22592, 1024).rearrange("p (k n) -> p k n", k=8)]
        Vhb = [wa(27648, 4160).rearrange("p (k n) -> p k n", k=32), wa(16384, 4160).rearrange("p (k n) -> p k n", k=32),
               XT[:].bitcast(BF16)[:, 0:4160].rearrange("p (k n) -> p k n", k=32)]
        Wvp = wa(35840, 2048).rearrange("p (k n) -> p k n", k=8)

        def load_wv(h):
            ld(Wvh[h % 2], w_in_v[:, :, 3072 + h * 128:3072 + (h + 1) * 128], ("Wvh", h % 2), eng="pool")

        def load_wqk(h):
            wb = h % 2
            ld(wqk[wb][:, :, 0:128], w_in_v[:, :, 1024 + h * 128:1024 + (h + 1) * 128], ("wq", wb), eng="pool")
            ld(wqk[wb][:, :, 128:256], w_in_v[:, :, 2048 + h * 128:2048 + (h + 1) * 128], ("wk", wb), eng="pool")

        if "A" in phases:
            Wuv = wa(0, 8192).rearrange("p (k n) -> p k n", k=8)
            Wgt = wa(8192, 16384).rearrange("p (k n) -> p k n", k=8)
            PA = wa(32768, 4096).rearrange("p (k n) -> p k n", k=4)
            WsT = wa(36864, 1024).rearrange("p (g t) -> p g t", g=8)
            load_wv(0)
            S.add("pool", lambda e: e.memset(Vhb[0][:, :, 128:130], 1.0), writes=[("Vones", 0)])
            load_wqk(0)
            ld(Wuv, w_in_v[:, :, 0:1024], "Wuv", eng="pool")
            ld(WsT, sg_wT[:, :, :], "WsT", eng="pool")
            ld(PA, w_proj_sg.rearrange("(kc p) n -> p kc n", p=128), "PA", eng="pool")
            for j in range(2):
                ld(Wgt[:, :, j * 1024:(j + 1) * 1024], w_in_v[:, :, 4096 + j * 1024:4096 + (j + 1) * 1024], ("Wgt", j), eng="pool")
            vg = [B32f[:, i * 512:(i + 1) * 512] for i in range(2)]
            ug = [B32[:, 4096 + i * 512:4096 + (i + 1) * 512] for i in range(2)]
            vln = [B32[:, 5120 + i * 512:5120 + (i + 1) * 512] for i in range(2)]
            osg = [B32[:, 6144 + i * 512:6144 + (i + 1) * 512] for i in range(2)]
            osgT = [B32[:, 7168 + i * 512:7168 + (i + 1) * 512].rearrange("p (k t) -> p k t", k=4) for i in range(2)]
            gsg = [B32[:, 8192 + i * 1024:8192 + (i + 1) * 1024] for i in range(2)]
            gda_t = [B32[:, 10240 + i * 1024:10240 + (i + 1) * 1024] for i in range(2)]
            zsg_t = [B32[:, 12288 + i * 1024:12288 + (i + 1) * 1024] for i in range(2)]
            lnst = [sbt("lnst%d" % i, [128, 4], F32) for i in range(2)]

            def front_load(t, sl):
                ld(xt[sl][:], x[t * 128:(t + 1) * 128, :], ("xt", sl))

            def front_compute(t, sl):
                S.add("act", lambda e: e.activation(out=xs[sl][:], in_=xt[sl][:], func=AF.Square, scale=1.0 / 32, accum_out=ss[sl][:]),
                      reads=[("xt", sl)], writes=[("xs", sl), ("ss", sl)])
                S.add("act", lambda e: e.activation(out=ss[sl][:], in_=ss[sl][:], func=AF.Sqrt, bias=epsb[:, 0:1], scale=1.0),
                      reads=[("ss", sl), "epsb"], writes=[("ss", sl)])
                S.add("dve", lambda e: e.reciprocal(out=ss[sl][:], in_=ss[sl][:]), reads=[("ss", sl)], writes=[("ss", sl)])
                S.add("dve", lambda e: e.tensor_scalar(out=xs[sl][:], in0=xt[sl][:], scalar1=ss[sl][:, 0:1], scalar2=None, op0=ALU.mult),
                      reads=[("xt", sl), ("ss", sl)], writes=[("xs", sl)])

            def tile_gen(t, sl, nxt_t):
                own = t < NT_OWN
                b0, b1, b2, b3 = 4 * sl, 4 * sl + 1, 4 * sl + 2, 4 * sl + 3
                P01 = PSP[2 * sl]
                P23 = PSP[2 * sl + 1]
                tcols = slice(t * 128, (t + 1) * 128)
                if nxt_t is not None:
                    front_load(nxt_t, sl)
                for kc in range(8):
                    S.add("pe", lambda e, kc=kc: e.matmul(bank(b0 + kc // 4)[:, (kc % 4) * 128:(kc % 4 + 1) * 128], lhsT=xs[sl][:, kc * 128:(kc + 1) * 128], rhs=ident[:], start=True, stop=True),
                          reads=[("xs", sl), "ident"], writes=[bk(b0 + kc // 4)])
                yield
                S.add("dve", lambda e: e.tensor_tensor(out=xnT[:, :, tcols], in0=P01[:].rearrange("p a (b t) -> p (a b) t", t=128),
                                                       in1=g1T[:].unsqueeze(2).broadcast_to([128, 8, 128]), op=ALU.mult),
                      reads=[bk(b0), bk(b1), "g1T"], writes=[("xnT", t)])
                if nxt_t is not None:
                    front_compute(nxt_t, sl)
                yield
                for kc in range(8):
                    S.add("pe", lambda e, kc=kc: e.matmul(bank(b2)[:, 0:128], lhsT=xnT[:, kc, tcols], rhs=Wvh[0][:, kc, :], start=(kc == 0), stop=(kc == 7)),
                          reads=[("xnT", t), ("Wvh", 0)], writes=[bk(b2)])
                yield
                S.add("dve", lambda e: e.tensor_copy(out=Vhb[0][:, t, 0:128], in_=bank(b2)[:, 0:128]), reads=[bk(b2)], writes=[("Vh", 0, t)])
                if not own:
                    return
                for j in range(2):
                    for kc in range(8):
                        S.add("pe", lambda e, kc=kc, j=j: e.matmul(bank(b0 + j), lhsT=xnT[:, kc, tcols], rhs=Wuv[:, kc, j * 512:(j + 1) * 512], start=(kc == 0), stop=(kc == 7)),
                              reads=[("xnT", t), "Wuv"], writes=[bk(b0 + j)])
                yield
                S.add("act", lambda e: e.activation(out=ug[sl][:], in_=bank(b0), func=AF.Gelu_apprx_tanh), reads=[bk(b0)], writes=[("ug", sl)])
                S.add("act", lambda e: e.activation(out=vg[sl][:], in_=bank(b1), func=AF.Gelu_apprx_tanh, accum_out=lnst[sl][:, 0:1]),
                      reads=[bk(b1)], writes=[("vg", sl), ("lnst", sl, 0)])
                yield
                for j in range(2):
                    for kc in range(8):
                        S.add("pe", lambda e, kc=kc, j=j: e.matmul(bank(b2 + j), lhsT=xnT[:, kc, tcols], rhs=Wgt[:, kc, j * 512:(j + 1) * 512], start=(kc == 0), stop=False),
                              reads=[("xnT", t), ("Wgt", 0)], writes=[bk(b2 + j)])
                    S.add("pe", lambda e, j=j: e.matmul(bank(b2 + j), lhsT=onesrow[0:1, 0:128], rhs=bgate_row[0:1, j * 512:(j + 1) * 512], start=False, stop=True),
                          reads=["onesrow", "bgate_row"], writes=[bk(b2 + j)])
                S.add("dve", lambda e: e.tensor_scalar(out=lnst[sl][:, 1:2], in0=lnst[sl][:, 0:1], scalar1=-1.0 / 512, scalar2=None, op0=ALU.mult),
                      reads=[("lnst", sl, 0)], writes=[("lnst", sl, 1)])
                yield
                S.add("act", lambda e: e.activation(out=vln[sl][:], in_=vg[sl][:], func=AF.Square, bias=lnst[sl][:, 1:2], scale=1.0, accum_out=lnst[sl][:, 2:3]),
                      reads=[("vg", sl), ("lnst", sl, 1)], writes=[("vln", sl), ("lnst", sl, 2)])
                S.add("act", lambda e: e.activation(out=lnst[sl][:, 2:3], in_=lnst[sl][:, 2:3], func=AF.Sqrt, bias=epsb[:, 0:1], scale=1.0 / 512),
                      reads=[("lnst", sl, 2), "epsb"], writes=[("lnst", sl, 2)])
                yield
                S.add("dve", lambda e: e.reciprocal(out=lnst[sl][:, 2:3], in_=lnst[sl][:, 2:3]), reads=[("lnst", sl, 2)], writes=[("lnst", sl, 2)])
                S.add("dve", lambda e: e.tensor_scalar(out=vg[sl][:], in0=vg[sl][:], scalar1=lnst[sl][:, 1:2], scalar2=lnst[sl][:, 2:3], op0=ALU.add, op1=ALU.mult),
                      reads=[("vg", sl), ("lnst", sl, 1), ("lnst", sl, 2)], writes=[("vg", sl)])
                S.add("dve", lambda e: e.tensor_tensor(out=vg[sl][:], in0=vg[sl][:], in1=lng_b[:], op=ALU.mult), reads=[("vg", sl), "lng_b"], writes=[("vg", sl)])
                S.add("dve", lambda e: e.tensor_tensor(out=vln[sl][:], in0=vg[sl][:], in1=lnb_b[:], op=ALU.add), reads=[("vg", sl), "lnb_b"], writes=[("vln", sl)])
                S.add("act", lambda e: e.activation(out=gsg[sl][:], in_=P23[:].rearrange("p a n -> p (a n)"), func=AF.Sigmoid),
                      reads=[bk(b2), bk(b3)], writes=[("gsg", sl)])
                yield
                S.add("pe", lambda e: e.matmul(bank(b0), lhsT=bs8, rhs=e8[:], start=True, stop=False, skip_group_check=True),
                      reads=["bs8", "e8"], writes=[bk(b0)])
                for g in range(8):
                    S.add("pe", lambda e, g=g: e.matmul(bank(b0)[:, g * 64:(g + 1) * 64], lhsT=WsT[:, g, :], rhs=vln[sl][:, g * 64:(g + 1) * 64], start=False, stop=(g == 7), skip_group_check=True),
                          reads=["WsT", ("vln", sl)], writes=[bk(b0)])
                for j in range(2):
                    for kc in range(8):
                        S.add("pe", lambda e, kc=kc, j=j: e.matmul(bank(b2 + j), lhsT=xnT[:, kc, tcols], rhs=Wgt[:, kc, 1024 + j * 512:1024 + (j + 1) * 512], start=(kc == 0), stop=False),
                              reads=[("xnT", t), ("Wgt", 1)], writes=[bk(b2 + j)])
                    S.add("pe", lambda e, j=j: e.matmul(bank(b2 + j), lhsT=onesrow[0:1, 0:128], rhs=bgate_row[0:1, 1024 + j * 512:1024 + (j + 1) * 512], start=False, stop=True),
                          reads=["onesrow", "bgate_row"], writes=[bk(b2 + j)])
                yield
                S.add("dve", lambda e: e.tensor_tensor(out=osg[sl][:], in0=bank(b0), in1=ug[sl][:], op=ALU.mult), reads=[bk(b0), ("ug", sl)], writes=[("osg", sl)])
                S.add("act", lambda e: e.activation(out=gda_t[sl][:], in_=P23[:].rearrange("p a n -> p (a n)"), func=AF.Sigmoid),
                      reads=[bk(b2), bk(b3)], writes=[("gda_t", sl)])
                S.add("pool", lambda e: e.dma_start(out=GDA[t * 128:(t + 1) * 128, :], in_=gda_t[sl][:]),
                      reads=[("gda_t", sl)], writes=[("GDA", t)], dma_sem=("gda_st", sl))
                yield
                for kc in range(4):
                    S.add("pe", lambda e, kc=kc: e.matmul(bank(b1)[:, kc * 128:(kc + 1) * 128], lhsT=osg[sl][:, kc * 128:(kc + 1) * 128], rhs=ident[:], start=True, stop=True),
                          reads=[("osg", sl), "ident"], writes=[bk(b1)])
                yield
                S.add("act", lambda e: e.copy(out=osgT[sl][:], in_=bank(b1).rearrange("p (k t) -> p k t", k=4)), reads=[bk(b1)], writes=[("osgT", sl)])
                yield
                for j in range(2):
                    for kc in range(4):
                        S.add("pe", lambda e, kc=kc, j=j: e.matmul(bank(b2 + j), lhsT=osgT[sl][:, kc, :], rhs=PA[:, kc, j * 512:(j + 1) * 512], start=(kc == 0), stop=(kc == 3)),
                              reads=[("osgT", sl), "PA"], writes=[bk(b2 + j)])
                yield
                S.add("dve", lambda e: e.tensor_tensor(out=zsg_t[sl][:], in0=P23[:].rearrange("p a n -> p (a n)"), in1=gsg[sl][:], op=ALU.mult),
                      reads=[bk(b2), bk(b3), ("gsg", sl)], writes=[("zsg_t", sl)])
                S.add("pool", lambda e: e.dma_start(out=ZSG[t * 128:(t + 1) * 128, :], in_=zsg_t[sl][:]),
                      reads=[("zsg_t", sl)], writes=[("ZSG", t)], dma_sem=("zsg_st", sl))

            order = list(range(NT_OWN, NT_ALL)) + list(range(NT_OWN))
            slot_tiles = [order[0::2], order[1::2]]

            def slot_gen(sl):
                tiles = slot_tiles[sl]
                for i, t in enumerate(tiles):
                    yield from tile_gen(t, sl, tiles[i + 1] if i + 1 < len(tiles) else None)

            for sl in range(2):
                front_load(slot_tiles[sl][0], sl)
                front_compute(slot_tiles[sl][0], sl)
            active = [slot_gen(0), slot_gen(1)]
            step = 0
            STAGGER = 0
            while any(a is not None for a in active):
                for sl in range(2):
                    if active[sl] is not None and (sl == 0 or step >= STAGGER):
                        try:
                            next(active[sl])
                        except StopIteration:
                            active[sl] = None
                step += 1
            S.barrier()

        if "B" in phases:
            kT = wa(0, 8192).rearrange("p (c t) -> p c t", c=2)
            qTL = wa(8192, 4096).rearrange("p (c t) -> p c t", c=2)
            qTR = wa(12288, 4096).rearrange("p (c t) -> p c t", c=2)
            PT = [wa(32768 + i * 1024, 1024).rearrange("p (c n) -> p c n", c=2) for i in range(3)]
            rk = [TA[:, i * 512:(i + 1) * 512] for i in range(2)]
            o4 = TA[:, 1024:1536].rearrange("p (q d) -> p q d", q=4)
            sq = [TAb[:, 3072 + i * 512:3072 + (i + 1) * 512] for i in range(2)]
            accs = TA[:, 2048:3488].rearrange("p (s d) -> p s d", s=9)
            rec9 = sbt("rec9", [128, 8], F32)
            ssall = sbt("ssall", [128, 16, 8], F32)

            ld(ones64[:], c_ones64[:, :], "ones64")
            ld(diag[:], c_diag[:, :], "diag")
            ld(gq_b[:], q_norm_g[0:1, :].broadcast_to([128, 64]), "gq_b")
            ld(gk_b[:], k_norm_g[0:1, :].broadcast_to([128, 64]), "gk_b")
            for i in range(2):
                ld(gqT[64 * i:64 * i + 64, :], q_norm_g[0].rearrange("(d o) -> d o", o=1), ("gqT", i), allow_slow_non_contiguous=True)
                ld(gkT[64 * i:64 * i + 64, :], k_norm_g[0].rearrange("(d o) -> d o", o=1), ("gkT", i), allow_slow_non_contiguous=True)
            for i, lv in enumerate((lam_q1, lam_k1, lam_q2, lam_k2)):
                ld(lam4[:, i, :], lv[0:1, :].broadcast_to([128, 64]), ("lam4", i))
            ld(sublng[:], subln_g[0:1, :].broadcast_to([128, 128]), "sublng")
            for h in range(H):
                S.add("dve", lambda e, h=h: e.tensor_scalar(out=gqs[:, h:h + 1], in0=gqT[:], scalar1=float(2.0 ** (h - 2)), scalar2=None, op0=ALU.mult),
                      reads=[("gqT", 0), ("gqT", 1)], writes=[("gqs", h)])
            S.add("dve", lambda e: e.tensor_tensor(out=lamt[:, 0, :], in0=lam4[:, 0, :], in1=lam4[:, 1, :], op=ALU.mult),
                  reads=[("lam4", 0), ("lam4", 1)], writes=[("lamt", 0)])
            S.add("dve", lambda e: e.tensor_tensor(out=lamt[:, 1, :], in0=lam4[:, 2, :], in1=lam4[:, 3, :], op=ALU.mult),
                  reads=[("lam4", 2), ("lam4", 3)], writes=[("lamt", 1)])
            S.add("dve", lambda e: e.reduce_sum(out=lams[:], in_=lamt[:], axis=mybir.AxisListType.X),
                  reads=[("lamt", 0), ("lamt", 1)], writes=["lams"])
            S.add("act", lambda e: e.activation(out=lams[:], in_=lams[:], func=AF.Exp), reads=["lams"], writes=["lams"])
            S.add("dve", lambda e: e.tensor_tensor(out=neglam[:], in0=lams[:, 1:2], in1=lams[:, 0:1], op=ALU.subtract),
                  reads=["lams"], writes=["neglam"])
            S.add("dve", lambda e: e.tensor_scalar(out=neglam[:], in0=neglam[:], scalar1=-float(LAM_INIT), scalar2=None, op0=ALU.add),
                  reads=["neglam"], writes=["neglam"])
            S.add("dve", lambda e: e.reduce_max(out=cmax[:, 0:1], in_=gq_b[:], axis=mybir.AxisListType.X, apply_absolute_value=True),
                  reads=["gq_b"], writes=[("cmax", 0)])
            S.add("dve", lambda e: e.reduce_max(out=cmax[:, 1:2], in_=gk_b[:], axis=mybir.AxisListType.X, apply_absolute_value=True),
                  reads=["gk_b"], writes=[("cmax", 1)])
            S.add("dve", lambda e: e.scalar_tensor_tensor(out=negc[:], in0=cmax[:, 0:1], scalar=-8.0, in1=cmax[:, 1:2], op0=ALU.mult, op1=ALU.mult),
                  reads=[("cmax", 0), ("cmax", 1)], writes=["negc"])
            S.add("dve", lambda e: e.tensor_scalar(out=sublng[:], in0=sublng[:], scalar1=float(1.0 - LAM_INIT), scalar2=None, op0=ALU.mult),
                  reads=["sublng"], writes=["sublng"])

            for (tl, nm) in ((kT, "kaug"), (qTL, "qaugL"), (qTR, "qaugR")):
                S.add("pool", lambda e, tl=tl: e.memset(tl[64:96, 0, :], 0.0), writes=[(nm, 0)])
                S.add("pool", lambda e, tl=tl: e.memset(tl[0:64, 1, :], 0.0), writes=[(nm, 1)])
            for c in range(2):
                r0 = 64 if c == 0 else 0
                ld(kT[r0:r0 + 4, c, :], c_kaug[:, :], ("kaug", c))
                ld(qTL[r0:r0 + 4, c, :], c_qaugL[:, :], ("qaugL", c))
                ld(qTR[r0:r0 + 4, c, :], c_qaugR[:, :], ("qaugR", c))
            for i in (1, 2):
                S.add("pool", lambda e, i=i: e.memset(Vhb[i][:, :, 128:130], 1.0), writes=[("Vones", i)])

            def fill_heads(h):
                return [j for j in (h + 1, h + 2) if j < H] if h % 2 == 0 else []

            def load_wvp(h):
                hs = fill_heads(h)
                if hs:
                    n = 128 * len(hs)
                    ld(Wvp[:, :, 0:n], w_in_v[:, :, 3072 + hs[0] * 128:3072 + hs[0] * 128 + n], "Wvp", eng="pool")

            def vfill_items(h):
                hs = fill_heads(h)
                nh = len(hs)
                items = []
                if nh == 0:
                    return items
                for g in range(16):
                    for q in range(2):
                        kb = 2 * g + q
                        reg = bank(7)[:, q * 128 * nh:(q + 1) * 128 * nh]
                        for kc in range(8):
                            def mm(kb=kb, kc=kc, reg=reg, q=q, g=g):
                                S.add("pe", lambda e: e.matmul(reg, lhsT=xnT[:, kc, kb * 128:(kb + 1) * 128], rhs=Wvp[:, kc, 0:128 * nh], start=(kc == 0), stop=(kc == 7), skip_group_check=True),
                                      reads=[("xnT", kb), "Wvp"], writes=[bk(7)])
                                if q == 1 and kc == 7:
                                    src = bank(7)[:, 0:256 * nh].rearrange("p (q j n) -> p q j n", q=2, j=nh)
                                    for j, hj in enumerate(hs):
                                        S.add("dve", lambda e, j=j, hj=hj: e.tensor_copy(out=Vhb[hj % 3][:, 2 * g:2 * g + 2, 0:128], in_=src[:, :, j, :]),
                                              reads=[bk(7)], writes=[("Vh", hj % 3, 2 * g), ("Vh", hj % 3, 2 * g + 1)])
                            items.append(mm)
                return items

            load_wvp(0)

            accb = [4, 5, 6]

            def acc(c, qi):
                idx = c * 4 + qi
                return bank(accb[idx // 3])[:, (idx % 3) * 160:(idx % 3) * 160 + 129]

            def acck(c, qi):
                return bk(accb[(c * 4 + qi) // 3])


            def do_head(h):
                wb = h % 2
                Vh = Vhb[h % 3]
                vfill = vfill_items(h)
                if h % 2 == 1 and h + 1 < H:
                    load_wvp(h + 1)
                groups = [(0, tg) for tg in range(8)] + [(1, tg) for tg in range(4)]

                def proj_pe(s_):
                    isq, tg = groups[s_]
                    pK = bank(s_ % 4)
                    col0 = 128 if isq == 0 else 0
                    wkey = ("wk", wb) if isq == 0 else ("wq", wb)
                    for kc in range(8):
                        S.add("pe", lambda e, kc=kc: e.matmul(pK, lhsT=wqk[wb][:, kc, col0:col0 + 128], rhs=xnT[:, kc, tg * 512:(tg + 1) * 512], start=(kc == 0), stop=(kc == 7)),
                              reads=[wkey] + [("xnT", 4 * tg + i) for i in range(4)], writes=[bk(s_ % 4)])
                    S.add("act", lambda e: e.activation(out=sq[s_ % 2][:], in_=pK, func=AF.Square),
                          reads=[bk(s_ % 4)], writes=[("sq", s_ % 2)])

                def proj_post(s_):
                    isq, tg = groups[s_]
                    pK = bank(s_ % 4)
                    pSS = bank(4 + s_ % 2)
                    pb = s_ % 2
                    tcols = slice(tg * 512, (tg + 1) * 512)
                    S.add("pe", lambda e: e.matmul(pSS, lhsT=ones64[:], rhs=sq[pb][:], start=True, stop=True),
                          reads=["ones64", ("sq", pb)], writes=[bk(4 + pb)])
                    S.add("act", lambda e: e.activation(out=rk[pb][:], in_=pSS, func=AF.Ln, bias=epsb[:, 0:1], scale=1.0),
                          reads=[bk(4 + pb), "epsb"], writes=[("rk", pb)])
                    S.add("act", lambda e: e.activation(out=rk[pb][:], in_=rk[pb][:], func=AF.Exp, scale=-0.5),
                          reads=[("rk", pb)], writes=[("rk", pb)])
                    for c in range(2):
                        rows = slice(64 * c, 64 * c + 64)
                        if isq == 0:
                            S.add("dve", lambda e, c=c, rows=rows: e.scalar_tensor_tensor(out=kT[rows, c, tcols], in0=pK[rows, :], scalar=gkT[rows, 0:1], in1=rk[pb][rows, :], op0=ALU.mult, op1=ALU.mult),
                                  reads=[bk(s_ % 4), ("rk", pb), ("gkT", c)], writes=[("kT", c, tg)])
                        else:
                            S.add("dve", lambda e, c=c, rows=rows: e.scalar_tensor_tensor(out=qTL[rows, c, tcols], in0=pK[rows, :], scalar=gqs[rows, h:h + 1], in1=rk[pb][rows, :], op0=ALU.mult, op1=ALU.mult),
                                  reads=[bk(s_ % 4), ("rk", pb), ("gqs", h)], writes=[("qTL", c, tg)])
                            S.add("pool", lambda e, c=c, rows=rows: e.tensor_copy(out=qTR[rows, c, tcols], in_=qTL[rows, c, tcols]),
                                  reads=[("qTL", c, tg)], writes=[("qTR", c, tg)])

                for s_ in range(len(groups) + 1):
                    if s_ < len(groups):
                        proj_pe(s_)
                    if s_ >= 1:
                        proj_post(s_ - 1)
                if h + 1 < H:
                    load_wqk(h + 1)

                slope = float(2.0 ** (-(h + 1)))
                iters = [(G, kb) for G in range(4) for kb in range(32)]

                def emit_qk(it):
                    G, kb = iters[it]
                    sb = it % 2
                    pS = PSP[sb]
                    late = []
                    for c in range(2):
                        kkey = ("kT", c, kb // 4)
                        kaugk = ("kaug", c)
                        kcols = slice(kb * 128, (kb + 1) * 128)

                        rall = slice(0, 96) if c == 0 else slice(0, 128)
                        rdat = slice(0, 64) if c == 0 else slice(64, 128)

                        def mmL(q0, q1, c=c, kcols=kcols, rall=rall):
                            return lambda e: e.matmul(pS[:, c, q0 * 128:q1 * 128], lhsT=kT[rall, c, kcols], rhs=qTL[rall, c, (4 * G + q0) * 128:(4 * G + q1) * 128], start=True, stop=True)

                        def mmR(q0, q1, c=c, kcols=kcols, rall=rall):
                            return lambda e: e.matmul(pS[:, c, q0 * 128:q1 * 128], lhsT=kT[rall, c, kcols], rhs=qTR[rall, c, (4 * G + q0) * 128:(4 * G + q1) * 128], start=True, stop=True)

                        rdL = [kkey, kaugk, ("qTL", c, G), ("qaugL", c)]
                        rdR = [kkey, kaugk, ("qTR", c, G), ("qaugR", c)]
                        wr = [bk(2 * sb + c)]
                        if kb >= 16 or kb < 4 * G:
                            S.add("pe", mmL(0, 4), reads=rdL, writes=wr)
                        elif kb > 4 * G + 3:
                            S.add("pe", mmR(0, 4), reads=rdR, writes=wr)
                        else:
                            d = kb - 4 * G
                            if d > 0:
                                S.add("pe", mmR(0, d), reads=rdR, writes=wr)
                            S.add("pe", lambda e, c=c, kcols=kcols, d=d, rall=rall: e.matmul(pS[:, c, d * 128:512], lhsT=kT[rall, c, kcols], rhs=qTL[rall, c, (4 * G + d) * 128:(4 * G + 4) * 128], start=True, stop=False, skip_group_check=True),
                                  reads=rdL, writes=wr)
                            late.append((lambda e, c=c, d=d: e.matmul(pS[:, c, d * 128:(d + 1) * 128], lhsT=ident[:], rhs=diag[:], start=False, stop=True, skip_group_check=True), wr))

                    for fn_, wr_ in late:
                        S.add("pe", fn_, reads=["ident", "diag"], writes=wr_)

                def emit_exp_av(it):
                    G, kb = iters[it]
                    sb = it % 2
                    pt = it % 3
                    pS = PSP[sb]
                    S.add("act", lambda e: e.activation(out=PT[pt][:], in_=pS[:], func=AF.Exp, bias=negc[:, 0:1], scale=slope),
                          reads=[bk(2 * sb), bk(2 * sb + 1), "negc"], writes=[("PT", pt)])
                    for c in range(2):
                        for qi in range(4):
                            S.add("pe", lambda e, c=c, qi=qi: e.matmul(acc(c, qi), lhsT=PT[pt][:, c, qi * 128:(qi + 1) * 128], rhs=Vh[:, kb, 0:129], start=(kb == 0 and (c * 4 + qi) % 3 == 0), stop=(kb == 31), skip_group_check=True),
                                  reads=[("PT", pt), ("Vh", h % 3, kb), ("Vones", h % 3)], writes=[acck(c, qi)])

                def emit_epilogue(G):
                    for b3 in range(3):
                        ns = 3 if b3 < 2 else 2
                        S.add("dve", lambda e, b3=b3, ns=ns: e.tensor_copy(out=accs[:, 3 * b3:3 * b3 + ns, 0:129], in_=bank(accb[b3])[:, 0:160 * ns].rearrange("p (s d) -> p s d", s=ns)[:, :, 0:129]),
                              reads=[bk(accb[b3])], writes=[("accs", b3)])
                    ak = [("accs", 0), ("accs", 1), ("accs", 2)]
                    S.add("dve", lambda e: e.reciprocal(out=rec9[:].unsqueeze(2), in_=accs[:, 0:8, 128:129]), reads=ak, writes=["rec9"])
                    S.add("dve", lambda e: e.tensor_scalar(out=rec9[:, 4:8], in0=rec9[:, 4:8], scalar1=neglam[:, 0:1], scalar2=None, op0=ALU.mult),
                          reads=["rec9", "neglam"], writes=["rec9"])
                    S.add("dve", lambda e: e.tensor_tensor(out=accs[:, 0:8, 0:128], in0=accs[:, 0:8, 0:128], in1=rec9[:].unsqueeze(2).broadcast_to([128, 8, 128]), op=ALU.mult),
                          reads=ak + ["rec9"], writes=ak)
                    S.add("dve", lambda e: e.tensor_tensor(out=o4[:], in0=accs[:, 0:4, 0:128], in1=accs[:, 4:8, 0:128], op=ALU.add), reads=ak, writes=["o4"])
                    S.add("dve", lambda e: e.tensor_tensor(out=accs[:, 0:4, 0:128], in0=o4[:], in1=o4[:], op=ALU.mult), reads=["o4"], writes=ak)
                    S.add("dve", lambda e: e.reduce_sum(out=ssall[:, 4 * G:4 * G + 4, h], in_=accs[:, 0:4, 0:128], axis=mybir.AxisListType.X),
                          reads=ak, writes=[("ssall", h, G)])
                    S.add("pool", lambda e: e.tensor_copy(out=oda[:, 4 * G:4 * G + 4, h * 128:(h + 1) * 128], in_=o4[:]),
                          reads=["o4"], writes=[("B32", 4 * G + q) for q in range(4)])

                emit_qk(0)
                for it in range(len(iters)):
                    if it + 1 < len(iters):
                        emit_qk(it + 1)
                    for _ in range(2):
                        if vfill:
                            vfill.pop(0)()
                    emit_exp_av(it)
                    if iters[it][1] == 31:
                        emit_epilogue(iters[it][0])

            for h_ in range(H):
                do_head(h_)

            allss = [("ssall", h, G) for h in range(H) for G in range(4)]
            allb = [("B32", t) for t in range(NT_OWN)]
            S.add("act", lambda e: e.activation(out=ssall[:], in_=ssall[:], func=AF.Sqrt, bias=epsb[:, 0:1], scale=1.0 / 128),
                  reads=allss + ["epsb"], writes=allss)
            S.add("dve", lambda e: e.reciprocal(out=ssall[:], in_=ssall[:]), reads=allss, writes=allss + ["ssall_r"])
            if dbg:
                for t in range(NT_OWN):
                    S.add("sp", lambda e, t=t: e.dma_start(out=ODA[t * 128:(t + 1) * 128, :], in_=oda[:, t, :]), reads=[("B32", t)], dma_sem=("oda_dbg", t), final=True)
            S.barrier()

        if "C" in phases:
            PB = wa(0, 8192).rearrange("p (k n) -> p k n", k=8)
            Wout = wa(8192, 8192).rearrange("p (k n) -> p k n", k=8)
            RB = [18432, 0]

            def ffn_views(i):
                base = RB[i % 2]
                nf = FBLOCKS[i][1]
                wg = wa(base, 6144).rearrange("p (k n) -> p k n", k=8)
                wu = wa(base + 6144, 6144).rearrange("p (k n) -> p k n", k=8)
                wd = wa(base + 12288, 6144).rearrange("p (f n) -> p f n", f=6)
                return wg, wu, wd, nf

            def rpart(i, part):
                return ("Rp", 1 if RB[i % 2] else 0, part)

            aT = TAb[:, 2048:5120].rearrange("p (f n) -> p f n", f=6)
            ld(PB, w_proj_da.rearrange("(kc p) n -> p kc n", p=128), "PB", eng="pool")
            ld(Wout, w_out.rearrange("(kc p) n -> p kc n", p=128), "Wout", eng="pool")

            def load_ffn(i):
                wg, wu, wd, nf = ffn_views(i)
                f0 = FBLOCKS[i][0]
                extra = ["PB", "Wout"] if RB[i % 2] == 0 else []
                r = 1 if RB[i % 2] else 0
                S.add("pool", lambda e: e.dma_start(out=wg[:, :, 0:nf * 128], in_=w_ffn_gate.rearrange("(kc p) n -> p kc n", p=128)[:, :, f0 * 128:(f0 + nf) * 128]),
                      writes=[rpart(i, 0)] + extra, dma_sem=("ffn", r, 0))
                S.add("pool", lambda e: e.dma_start(out=wu[:, :, 0:nf * 128], in_=w_ffn_up.rearrange("(kc p) n -> p kc n", p=128)[:, :, f0 * 128:(f0 + nf) * 128]),
                      writes=[rpart(i, 1)] + extra, dma_sem=("ffn", r, 1))
                S.add("pool", lambda e: e.dma_start(out=wd[:, 0:nf, :], in_=w_ffn_down.rearrange("(f p) n -> p f n", p=128)[:, f0:f0 + nf, :]),
                      writes=[rpart(i, 2)] + extra, dma_sem=("ffn", r, 2))

            load_ffn(0)
            zf = TA[:, 0:1024]
            odaT = TAb[:, 2048:3072].rearrange("p (k t) -> p k t", k=8)
            zl = [TAb[:, 3072:4096]] * 2
            gl = [TAb[:, 4096:5120]] * 2
            zb = TAb[:, 5120:6144]
            zT = TAb[:, 6144:7168].rearrange("p (k t) -> p k t", k=8)
            sig = [TA[:, i * 512:(i + 1) * 512] for i in range(2)]

            def c_a0(t):
                ld(hacc[:, t, :], x[t * 128:(t + 1) * 128, :], ("hacc", t))
                odat = oda[:, t, :].rearrange("p (h d) -> p h d", h=8)
                S.add("dve", lambda e: e.tensor_tensor(out=odat, in0=odat, in1=ssall[:, t, :].unsqueeze(2).broadcast_to([128, 8, 128]), op=ALU.mult),
                      reads=[("B32", t), "ssall_r"], writes=[("B32", t)])
                S.add("dve", lambda e: e.tensor_tensor(out=odat, in0=odat, in1=sublng[:].unsqueeze(1).broadcast_to([128, 8, 128]), op=ALU.mult),
                      reads=[("B32", t), "sublng"], writes=[("B32", t)])

            def c_a1(t):
                ld(zl[0], ZSG[t * 128:(t + 1) * 128, :], ("zl", 0))
                ld(gl[0], GDA[t * 128:(t + 1) * 128, :], ("gl", 0))
                for hh in range(8):
                    S.add("pe", lambda e, hh=hh: e.matmul(bank(4 + hh // 4)[:, (hh % 4) * 128:(hh % 4 + 1) * 128], lhsT=oda[:, t, hh * 128:(hh + 1) * 128], rhs=ident[:], start=True, stop=True),
                          reads=[("B32", t), "ident"], writes=[bk(4 + hh // 4)])
                S.add("act", lambda e: e.copy(out=odaT[:], in_=PSP[2][:].rearrange("p a (b t) -> p (a b) t", t=128)), reads=[bk(4), bk(5)], writes=["odaT"])

            def c_a2(t):
                for j in range(2):
                    for hh in range(8):
                        S.add("pe", lambda e, hh=hh, j=j: e.matmul(bank(2 + j), lhsT=odaT[:, hh, :], rhs=PB[:, hh, j * 512:(j + 1) * 512], start=(hh == 0), stop=(hh == 7)),
                              reads=["odaT", "PB"], writes=[bk(2 + j)])
                S.add("dve", lambda e: e.tensor_tensor(out=zf[:], in0=PSP[1][:].rearrange("p a n -> p (a n)"), in1=gl[0], op=ALU.mult),
                      reads=[bk(2), bk(3), ("gl", 0)], writes=["zf"])
                S.add("dve", lambda e: e.tensor_tensor(out=zb[:], in0=zf[:], in1=zl[0], op=ALU.add), reads=["zf", ("zl", 0)], writes=["zb"])

            def c_a3(t):
                for kc in range(8):
                    S.add("pe", lambda e, kc=kc: e.matmul(bank(kc // 4)[:, (kc % 4) * 128:(kc % 4 + 1) * 128], lhsT=zb[:, kc * 128:(kc + 1) * 128], rhs=ident[:], start=True, stop=True),
                          reads=["zb", "ident"], writes=[bk(kc // 4)])
                S.add("act", lambda e: e.copy(out=zT[:], in_=PSP[0][:].rearrange("p a (b t) -> p (a b) t", t=128)), reads=[bk(0), bk(1)], writes=["zT"])

            def c_a4(t):
                for j in range(2):
                    for kc in range(8):
                        S.add("pe", lambda e, kc=kc, j=j: e.matmul(bank(6 + j), lhsT=zT[:, kc, :], rhs=Wout[:, kc, j * 512:(j + 1) * 512], start=(kc == 0), stop=(kc == 7)),
                              reads=["zT", "Wout"], writes=[bk(6 + j)])
                S.add("dve", lambda e: e.tensor_tensor(out=hacc[:, t, :], in0=PSP[3][:].rearrange("p a n -> p (a n)"), in1=hacc[:, t, :], op=ALU.add),
                      reads=[bk(6), bk(7), ("hacc", t)], writes=[("hacc", t)])

            def c_b1(t):
                rms_stage1(hacc[:, t, :], ("hacc", t), t % 2)

            def c_b2(t):
                rms_stage2(g2T, "g2T", hnT[:, t, :, :], ("B32", t), t % 2)

            stages = [c_a0, c_a1, c_a2, c_a3, c_a4, c_b1, c_b2]
            for s_ in range(NT_OWN + len(stages) - 1):
                for k in reversed(range(len(stages))):
                    t = s_ - k
                    if 0 <= t < NT_OWN:
                        stages[k](t)

            for i in range(len(FBLOCKS)):
                wg, wu, wd, nf = ffn_views(i)
                if i + 1 < len(FBLOCKS):
                    load_ffn(i + 1)
                for G in range(4):
                    for f in range(nf):
                        pb = f % 2
                        for (wmat, wkey, bi) in ((wg, rpart(i, 0), 0), (wu, rpart(i, 1), 1)):
                            for kc in range(8):
                                S.add("pe", lambda e, kc=kc, f=f, wmat=wmat, pb=pb, bi=bi, G=G: e.matmul(bank(2 * pb + bi), lhsT=wmat[:, kc, f * 128:(f + 1) * 128], rhs=hnT[:, 4 * G:4 * G + 4, kc, :], start=(kc == 0), stop=(kc == 7)),
                                      reads=[wkey] + [("B32", 4 * G + q) for q in range(4)], writes=[bk(2 * pb + bi)])
                        S.add("act", lambda e, pb=pb: e.activation(out=sig[pb][:], in_=bank(2 * pb), func=AF.Sigmoid), reads=[bk(2 * pb)], writes=[("sig", pb), "zf"])
                        S.add("dve", lambda e, pb=pb: e.tensor_tensor(out=sig[pb][:], in0=bank(2 * pb), in1=sig[pb][:], op=ALU.mult),
                              reads=[bk(2 * pb), ("sig", pb)], writes=[("sig", pb)])
                        S.add("dve", lambda e, pb=pb, f=f: e.tensor_tensor(out=aT[:, f, :], in0=bank(2 * pb + 1), in1=sig[pb][:], op=ALU.mult),
                              reads=[bk(2 * pb + 1), ("sig", pb)], writes=[("aT", f), "odaT", ("zl", 0), ("gl", 0)])
                    for ti in range(4):
                        t = 4 * G + ti
                        db = ti % 2
                        for j in range(2):
                            for f in range(nf):
                                S.add("pe", lambda e, f=f, j=j, ti=ti, db=db, wd=wd, nf=nf: e.matmul(bank(4 + 2 * db + j), lhsT=aT[:, f, ti * 128:(ti + 1) * 128], rhs=wd[:, f, j * 512:(j + 1) * 512], start=(f == 0), stop=(f == nf - 1)),
                                      reads=[("aT", f), rpart(i, 2)], writes=[bk(4 + 2 * db + j)])
                        S.add("dve", lambda e, t=t, db=db: e.tensor_tensor(out=hacc[:, t, :], in0=PSP[2 + db][:].rearrange("p a n -> p (a n)"), in1=hacc[:, t, :], op=ALU.add),
                              reads=[bk(4 + 2 * db), bk(5 + 2 * db), ("hacc", t)], writes=[("hacc", t)])
                        if i == len(FBLOCKS) - 1:
                            S.add("sp", lambda e, t=t: e.dma_start(out=out[t * 128:(t + 1) * 128, :], in_=hacc[:, t, :]),
                                  reads=[("hacc", t)], dma_sem=("out", t), final=True)

        if dbg and "A" in phases:
            pass
        nsem = S.emit(nc, st)
    return nc


def _bf16(a):
    return np.asarray(a, dtype=np.float32).astype(ml_dtypes.bfloat16)


def make_consts(half):
    ident = _bf16(np.eye(128))
    ones64 = np.zeros((128, 128), np.float32)
    ones64[:64, :64] = 1.0 / 64
    ones64[64:, 64:] = 1.0 / 64
    ones64 = _bf16(ones64)
    onesrow = _bf16(np.ones((1, 512)))
    ii = np.arange(128)
    diag = _bf16(-2.0 * np.maximum(ii[:, None] - ii[None, :], 0))
    own_blocks = half * 16 + np.arange(16)
    oth_blocks = (1 - half) * 16 + np.arange(16)
    kblocks = np.concatenate([own_blocks, oth_blocks])
    sign = np.concatenate([np.ones(16), np.full(16, 1.0 if half == 1 else -1.0)])
    kaug = np.zeros((4, S_ALL), np.float32)
    for kb in range(32):
        sl = slice(kb * 128, (kb + 1) * 128)
        kaug[0, sl] = sign[kb] * 128.0 * kblocks[kb]
        kaug[1, sl] = sign[kb] * ii
        kaug[2, sl] = sign[kb]
        kaug[3, sl] = sign[kb]
    qL = np.zeros((4, S_OWN), np.float32)
    for qb in range(16):
        sl = slice(qb * 128, (qb + 1) * 128)
        qL[0, sl] = 1.0
        qL[1, sl] = 1.0
        qL[2, sl] = -128.0 * own_blocks[qb]
        qL[3, sl] = -ii
    e8 = np.zeros((8, 512), np.float32)
    for g in range(8):
        e8[g, g * 64:(g + 1) * 64] = 1.0
    return {"c_ident": ident, "c_ones64": ones64, "c_onesrow": onesrow, "c_diag": diag, "c_e8": _bf16(e8),
            "c_kaug": _bf16(kaug), "c_qaugL": _bf16(qL), "c_qaugR": _bf16(-qL)}


def make_in_maps(inputs):
    f = lambda a: np.ascontiguousarray(np.asarray(a, dtype=np.float32))
    x = f(inputs["x"])
    shared = {
        "norm1_g": f(inputs["norm1_g"]).reshape(1, D),
        "w_in": f(inputs["w_in"])[0],
        "b_gate": f(inputs["b_gate"]).reshape(1, 2048),
        "sg_ln_g": f(inputs["sg_ln_g"]).reshape(1, 512),
        "sg_ln_b": f(inputs["sg_ln_b"]).reshape(1, 512),
        "sg_wT": np.ascontiguousarray(f(inputs["sg_w"])[0].transpose(2, 0, 1)),
        "sg_b": f(inputs["sg_b"]).reshape(8, 128),
        "q_norm_g": f(inputs["q_norm_g"]).reshape(1, 64),
        "k_norm_g": f(inputs["k_norm_g"]).reshape(1, 64),
        "lam_q1": f(inputs["lam_q1"]).reshape(1, 64),
        "lam_k1": f(inputs["lam_k1"]).reshape(1, 64),
        "lam_q2": f(inputs["lam_q2"]).reshape(1, 64),
        "lam_k2": f(inputs["lam_k2"]).reshape(1, 64),
        "subln_g": f(inputs["subln_g"]).reshape(1, 128),
        "w_proj_sg": f(inputs["w_proj_sg"])[0],
        "w_proj_da": f(inputs["w_proj_da"])[0],
        "w_out": f(inputs["w_out"])[0],
        "norm2_g": f(inputs["norm2_g"]).reshape(1, D),
        "w_ffn_gate": f(inputs["w_ffn_gate"])[0],
        "w_ffn_up": f(inputs["w_ffn_up"])[0],
        "w_ffn_down": f(inputs["w_ffn_down"])[0],
    }
    consts = [make_consts(0), make_consts(1)]
    in_maps = []
    for c in range(8):
        b, half = divmod(c, 2)
        own = x[b, half * S_OWN:(half + 1) * S_OWN]
        oth = x[b, (1 - half) * S_OWN:(2 - half) * S_OWN]
        m = dict(shared)
        m["x"] = np.ascontiguousarray(np.concatenate([own, oth], axis=0))
        m.update(consts[half])
        in_maps.append(m)
    return in_maps


def kernel(**inputs):
    nc = build_program()
    in_maps = make_in_maps(inputs)
    res = run_bass_kernel_spmd(nc, in_maps, core_ids=list(range(8)))
    out = np.empty((4, S_ALL, D), np.float32)
    for c in range(8):
        b, half = divmod(c, 2)
        out[b, half * S_OWN:(half + 1) * S_OWN] = np.asarray(res.results[c]["out"], dtype=np.float32)
    return out
```

```python
import os
import math
import numpy as np
import ml_dtypes
from contextlib import ExitStack
import concourse.bass as bass
import concourse.mybir as mybir
from concourse.bass_utils import run_bass_kernel_spmd

F32 = mybir.dt.float32
BF16 = mybir.dt.bfloat16
AF = mybir.ActivationFunctionType
ALU = mybir.AluOpType

EPS = 1e-6
D = 1024
S_OWN = 2048
S_ALL = 4096
NT_OWN = 16
NT_ALL = 32
H = 8
DFF = 2816
LAM_INIT = 0.8 - 0.6 * math.exp(-0.3 * 0)
FBLOCKS = [(0, 6), (6, 6), (12, 5), (17, 5)]


class Sched:
    ENGS = ("pe", "act", "dve", "pool", "sp")

    def __init__(self):
        self.q = {e: [] for e in self.ENGS}
        self.ncomp = {e: 0 for e in self.ENGS}
        self.lastw = {}
        self.readers = {}
        self.dma_cnt = {}
        self.final_tokens = []
        self.pending = {e: None for e in self.ENGS}

    def barrier(self):
        toks = {}
        for e in self.ENGS:
            if self.ncomp[e] > 0:
                toks[("eng", e)] = self.ncomp[e]
        for k, v in self.dma_cnt.items():
            toks[("dma", k)] = v
        for e in self.ENGS:
            self.pending[e] = dict(toks)

    def add(self, eng, fn, reads=(), writes=(), dma_sem=None, final=False):
        waits = {}

        def need(tok):
            if tok is None:
                return
            s, v, teng = tok
            if teng == eng and eng == "pe" and dma_sem is None:
                return
            if waits.get(s, 0) < v:
                waits[s] = v

        if self.pending[eng]:
            for s, v in self.pending[eng].items():
                if s == ("eng", "pe") and eng == "pe":
                    continue
                waits[s] = max(waits.get(s, 0), v)
            self.pending[eng] = None
        for k in reads:
            need(self.lastw.get(k))
        for k in writes:
            need(self.lastw.get(k))
            for s, (v, teng) in self.readers.get(k, {}).items():
                need((s, v, teng))
        if dma_sem is None:
            self.ncomp[eng] += 1
            tok = (("eng", eng), self.ncomp[eng], eng)
        else:
            self.dma_cnt[dma_sem] = self.dma_cnt.get(dma_sem, 0) + 16
            tok = (("dma", dma_sem), self.dma_cnt[dma_sem], None)
        for k in reads:
            d = self.readers.setdefault(k, {})
            if d.get(tok[0], (0, None))[0] < tok[1]:
                d[tok[0]] = (tok[1], tok[2])
        for k in writes:
            self.lastw[k] = tok
            self.readers[k] = {}
        self.q[eng].append((fn, waits, tok))
        if final:
            self.final_tokens.append(tok)
        return tok

    def emit(self, nc, stack):
        sems = {}

        def sem(key):
            if key not in sems:
                sems[key] = stack.enter_context(nc.semaphore("s%d" % len(sems)))
            return sems[key]

        for e in self.ENGS:
            for fn, waits, tok in self.q[e]:
                sem(tok[0])
        engobj = {"pe": "tensor", "act": "scalar", "dve": "vector", "pool": "gpsimd", "sp": "sync"}
        finals = list(self.final_tokens)
        with nc.Block() as block:
            for e in self.ENGS:
                ops = self.q[e]
                if not ops:
                    continue

                def body(eng, ops=ops, e=e):
                    waited = {}
                    for fn, waits, tok in ops:
                        pend = []
                        for s, v in waits.items():
                            if waited.get(s, 0) >= v:
                                continue
                            waited[s] = v
                            pend.append((s, v))
                        attach = None
                        if pend and tok[0][0] != "dma":
                            attach = pend.pop()
                        for s, v in pend:
                            eng.wait_ge(sem(s), v)
                        ins = fn(eng)
                        if attach is not None:
                            ins._wait_ge(sem(attach[0]), attach[1])
                        ins.then_inc(sem(tok[0]), 16 if tok[0][0] == "dma" else 1)
                    if e == "sp":
                        for tok in finals:
                            if waited.get(tok[0], 0) < tok[1]:
                                waited[tok[0]] = tok[1]
                                eng.wait_ge(sem(tok[0]), tok[1])

                getattr(block, engobj[e])(body)
        return len(sems)


def build_program(phases="ABC", dbg=False):
    nc = bass.Bass("TRN2", target_bir_lowering=False)

    def din(name, shape, dt=F32):
        return nc.dram_tensor(name, shape, dt, kind="ExternalInput").ap()

    x = din("x", [S_ALL, D])
    norm1_g = din("norm1_g", [1, D])
    w_in = din("w_in", [D, 6144])
    b_gate = din("b_gate", [1, 2048])
    sg_ln_g = din("sg_ln_g", [1, 512])
    sg_ln_b = din("sg_ln_b", [1, 512])
    sg_wT = din("sg_wT", [128, 8, 128])
    sg_b = din("sg_b", [8, 128])
    q_norm_g = din("q_norm_g", [1, 64])
    k_norm_g = din("k_norm_g", [1, 64])
    lam_q1 = din("lam_q1", [1, 64])
    lam_k1 = din("lam_k1", [1, 64])
    lam_q2 = din("lam_q2", [1, 64])
    lam_k2 = din("lam_k2", [1, 64])
    subln_g = din("subln_g", [1, 128])
    w_proj_sg = din("w_proj_sg", [512, D])
    w_proj_da = din("w_proj_da", [D, D])
    w_out = din("w_out", [D, D])
    norm2_g = din("norm2_g", [1, D])
    w_ffn_gate = din("w_ffn_gate", [D, DFF])
    w_ffn_up = din("w_ffn_up", [D, DFF])
    w_ffn_down = din("w_ffn_down", [DFF, D])
    c_ident = din("c_ident", [128, 128], BF16)
    c_ones64 = din("c_ones64", [128, 128], BF16)
    c_onesrow = din("c_onesrow", [1, 512], BF16)
    c_e8 = din("c_e8", [8, 512], BF16)
    c_diag = din("c_diag", [128, 128], BF16)
    c_kaug = din("c_kaug", [4, S_ALL], BF16)
    c_qaugL = din("c_qaugL", [4, S_OWN], BF16)
    c_qaugR = din("c_qaugR", [4, S_OWN], BF16)

    out = nc.dram_tensor("out", [S_OWN, D], F32, kind="ExternalOutput").ap()
    skind = "ExternalOutput" if dbg else "Internal"
    ZSG = nc.dram_tensor("zsg_scr", [S_OWN, D], BF16, kind=skind).ap()
    GDA = nc.dram_tensor("gda_scr", [S_OWN, D], BF16, kind=skind).ap()
    ODA = nc.dram_tensor("oda_dbg", [S_OWN, D], BF16, kind=skind).ap() if dbg else None

    w_in_v = w_in.rearrange("(kc p) n -> p kc n", p=128)

    S = Sched()
    with ExitStack() as st:
        def sbt(name, shape, dt):
            return st.enter_context(nc.sbuf_tensor(name, shape, dt))

        A64 = sbt("A64", [128, 32768], BF16)
        WA = sbt("WA", [128, 37888], BF16)
        B32 = sbt("B32", [128, 16384], BF16)
        TA = sbt("TA", [128, 3584], F32)
        B32f = B32[:].bitcast(F32)
        TAb = TA[:].bitcast(BF16)
        xnT = A64[:].rearrange("p (k t) -> p k t", k=8)
        hacc = A64[:].bitcast(F32).rearrange("p (t d) -> p t d", t=16)
        oda = B32[:].rearrange("p (t d) -> p t d", t=16)
        hnT = B32[:].rearrange("p (t k n) -> p t k n", t=16, k=8)

        def wa(off, n):
            return WA[:, off:off + n]

        PSP = [st.enter_context(nc.psum_tensor("psp%d" % i, [128, 2, 512], F32)) for i in range(4)]

        def bank(i):
            return PSP[i // 2][:, i % 2, :]

        def bk(i):
            return ("ps", i)

        ident = sbt("ident", [128, 128], BF16)
        ones64 = sbt("ones64", [128, 128], BF16)
        onesrow = sbt("onesrow", [1, 512], BF16)
        diag = sbt("diag", [128, 128], BF16)
        g1T = sbt("g1T", [128, 8], F32)
        g2T = sbt("g2T", [128, 8], F32)
        lng_b = B32f[:, 1024:1536]
        lnb_b = B32f[:, 1536:2048]
        bs8 = TAb[0:8, 0:128]
        e8 = sbt("e8", [8, 512], BF16)
        bgate_row = TAb[0:1, 1024:3072]
        epsb = sbt("epsb", [128, 1], F32)
        gq_b = sbt("gq_b", [128, 64], F32)
        gk_b = sbt("gk_b", [128, 64], F32)
        gqT = sbt("gqT", [128, 1], F32)
        gkT = sbt("gkT", [128, 1], F32)
        gqs = sbt("gqs", [128, 8], F32)
        lam4 = sbt("lam4", [128, 4, 64], F32)
        lamt = sbt("lamt", [128, 2, 64], F32)
        lams = sbt("lams", [128, 2], F32)
        neglam = sbt("neglam", [128, 1], F32)
        negc = sbt("negc", [128, 1], F32)
        cmax = sbt("cmax", [128, 2], F32)
        sublng = sbt("sublng", [128, 128], F32)

        def ld(dst_ap, src_ap, key, eng="sp", **kw):
            S.add(eng, lambda e: e.dma_start(out=dst_ap, in_=src_ap, **kw), writes=[key], dma_sem=key)

        ld(ident[:], c_ident[:, :], "ident")
        ld(onesrow[:], c_onesrow[:, :], "onesrow")
        ld(g1T[:], norm1_g[0].rearrange("(kc p) -> p kc", p=128), "g1T", allow_slow_non_contiguous=True)
        ld(g2T[:], norm2_g[0].rearrange("(kc p) -> p kc", p=128), "g2T", allow_slow_non_contiguous=True)
        ld(lng_b, sg_ln_g[0:1, :].broadcast_to([128, 512]), "lng_b")
        ld(lnb_b, sg_ln_b[0:1, :].broadcast_to([128, 512]), "lnb_b")
        ld(bs8, sg_b[:, :], "bs8", eng="pool")
        ld(e8[:], c_e8[:, :], "e8")
        ld(bgate_row, b_gate[:, :], "bgate_row", eng="pool")
        S.add("dve", lambda e: e.memset(epsb[:], EPS), writes=["epsb"])
        XT = sbt("XT", [128, 2080], F32)
        xt = [XT[:, 0:1024], XT[:, 1024:2048]]
        ss = [sbt("ss%d" % i, [128, 1], F32) for i in range(2)]
        xs = [sbt("xs%d" % i, [128, D], BF16) for i in range(2)]
        vt = [B32[:, 14336 + i * 1024:14336 + (i + 1) * 1024] for i in range(2)]

        def rms_stage1(src_ap, src_key, b):
            S.add("act", lambda e: e.activation(out=xs[b][:], in_=src_ap, func=AF.Square, scale=1.0 / 32, accum_out=ss[b][:]),
                  reads=[src_key], writes=[("xs", b), ("ss", b)])
            S.add("act", lambda e: e.activation(out=ss[b][:], in_=ss[b][:], func=AF.Sqrt, bias=epsb[:, 0:1], scale=1.0),
                  reads=[("ss", b), "epsb"], writes=[("ss", b)])
            S.add("dve", lambda e: e.reciprocal(out=ss[b][:], in_=ss[b][:]), reads=[("ss", b)], writes=[("ss", b)])
            S.add("dve", lambda e: e.tensor_scalar(out=xs[b][:], in0=src_ap, scalar1=ss[b][:, 0:1], scalar2=None, op0=ALU.mult),
                  reads=[src_key, ("ss", b)], writes=[("xs", b)])

        def rms_stage2(gT, gkey, dst_ap, dst_key, b):
            for kc in range(8):
                S.add("pe", lambda e, kc=kc: e.matmul(bank(kc // 4)[:, (kc % 4) * 128:(kc % 4 + 1) * 128], lhsT=xs[b][:, kc * 128:(kc + 1) * 128], rhs=ident[:], start=True, stop=True),
                      reads=[("xs", b), "ident"], writes=[bk(kc // 4)])
            S.add("dve", lambda e: e.tensor_tensor(out=dst_ap, in0=PSP[0][:].rearrange("p a (b t) -> p (a b) t", t=128),
                                                   in1=gT[:].unsqueeze(2).broadcast_to([128, 8, 128]), op=ALU.mult),
                  reads=[bk(0), bk(1), gkey], writes=[dst_key])

        wqk = [wa(24576, 2048).rearrange("p (k n) -> p k n", k=8), wa(20544, 2048).rearrange("p (k n) -> p k n", k=8)]
        Wvh = [wa(26624, 1024).rearrange("p (k n) -> p k n", k=8), wa(22592, 1024).rearrange("p (k n) -> p k n", k=8)]
        Vhb = [wa(27648, 4160).rearrange("p (k n) -> p k n", k=32), wa(16384, 4160).rearrange("p (k n) -> p k n", k=32),
               XT[:].bitcast(BF16)[:, 0:4160].rearrange("p (k n) -> p k n", k=32)]
        Wvp = wa(35840, 2048).rearrange("p (k n) -> p k n", k=8)

        def load_wv(h):
            ld(Wvh[h % 2], w_in_v[:, :, 3072 + h * 128:3072 + (h + 1) * 128], ("Wvh", h % 2), eng="pool")

        def load_wqk(h):
            wb = h % 2
            ld(wqk[wb][:, :, 0:128], w_in_v[:, :, 1024 + h * 128:1024 + (h + 1) * 128], ("wq", wb), eng="pool")
            ld(wqk[wb][:, :, 128:256], w_in_v[:, :, 2048 + h * 128:2048 + (h + 1) * 128], ("wk", wb), eng="pool")

        if "A" in phases:
            Wuv = wa(0, 8192).rearrange("p (k n) -> p k n", k=8)
            Wgt = wa(8192, 16384).rearrange("p (k n) -> p k n", k=8)
            PA = wa(32768, 4096).rearrange("p (k n) -> p k n", k=4)
            WsT = wa(36864, 1024).rearrange("p (g t) -> p g t", g=8)
            load_wv(0)
            S.add("pool", lambda e: e.memset(Vhb[0][:, :, 128:130], 1.0), writes=[("Vones", 0)])
            load_wqk(0)
            ld(Wuv, w_in_v[:, :, 0:1024], "Wuv", eng="pool")
            ld(WsT, sg_wT[:, :, :], "WsT", eng="pool")
            ld(PA, w_proj_sg.rearrange("(kc p) n -> p kc n", p=128), "PA", eng="pool")
            for j in range(2):
                ld(Wgt[:, :, j * 1024:(j + 1) * 1024], w_in_v[:, :, 4096 + j * 1024:4096 + (j + 1) * 1024], ("Wgt", j), eng="pool")
            vg = [B32f[:, i * 512:(i + 1) * 512] for i in range(2)]
            ug = [B32[:, 4096 + i * 512:4096 + (i + 1) * 512] for i in range(2)]
            vln = [B32[:, 5120 + i * 512:5120 + (i + 1) * 512] for i in range(2)]
            osg = [B32[:, 6144 + i * 512:6144 + (i + 1) * 512] for i in range(2)]
            osgT = [B32[:, 7168 + i * 512:7168 + (i + 1) * 512].rearrange("p (k t) -> p k t", k=4) for i in range(2)]
            gsg = [B32[:, 8192 + i * 1024:8192 + (i + 1) * 1024] for i in range(2)]
            gda_t = [B32[:, 10240 + i * 1024:10240 + (i + 1) * 1024] for i in range(2)]
            zsg_t = [B32[:, 12288 + i * 1024:12288 + (i + 1) * 1024] for i in range(2)]
            lnst = [sbt("lnst%d" % i, [128, 4], F32) for i in range(2)]

            xtb = [[xt[0], TA[:, 1536:2560]], [xt[1], TA[:, 2560:3584]]]

            def front_load(t, sl, par):
                ld(xtb[sl][par], x[t * 128:(t + 1) * 128, :], ("xt", sl, par))

            def front_compute(t, sl, par):
                xsrc = xtb[sl][par]
                S.add("act", lambda e: e.activation(out=xs[sl][:], in_=xsrc, func=AF.Square, scale=1.0 / 32, accum_out=ss[sl][:]),
                      reads=[("xt", sl, par)], writes=[("xs", sl), ("ss", sl)])
                S.add("act", lambda e: e.activation(out=ss[sl][:], in_=ss[sl][:], func=AF.Sqrt, bias=epsb[:, 0:1], scale=1.0),
                      reads=[("ss", sl), "epsb"], writes=[("ss", sl)])
                S.add("dve", lambda e: e.reciprocal(out=ss[sl][:], in_=ss[sl][:]), reads=[("ss", sl)], writes=[("ss", sl)])
                S.add("dve", lambda e: e.tensor_scalar(out=xs[sl][:], in0=xsrc, scalar1=ss[sl][:, 0:1], scalar2=None, op0=ALU.mult),
                      reads=[("xt", sl, par), ("ss", sl)], writes=[("xs", sl)])

            def tile_gen(t, sl, i, tiles):
                own = t < NT_OWN
                b0, b1, b2, b3 = 4 * sl, 4 * sl + 1, 4 * sl + 2, 4 * sl + 3
                P01 = PSP[2 * sl]
                P23 = PSP[2 * sl + 1]
                tcols = slice(t * 128, (t + 1) * 128)
                if i + 2 < len(tiles):
                    front_load(tiles[i + 2], sl, i % 2)
                for kc in range(8):
                    S.add("pe", lambda e, kc=kc: e.matmul(bank(b0 + kc // 4)[:, (kc % 4) * 128:(kc % 4 + 1) * 128], lhsT=xs[sl][:, kc * 128:(kc + 1) * 128], rhs=ident[:], start=True, stop=True),
                          reads=[("xs", sl), "ident"], writes=[bk(b0 + kc // 4)])
                yield
                S.add("dve", lambda e: e.tensor_tensor(out=xnT[:, :, tcols], in0=P01[:].rearrange("p a (b t) -> p (a b) t", t=128),
                                                       in1=g1T[:].unsqueeze(2).broadcast_to([128, 8, 128]), op=ALU.mult),
                      reads=[bk(b0), bk(b1), "g1T"], writes=[("xnT", t)])
                if i + 1 < len(tiles):
                    front_compute(tiles[i + 1], sl, (i + 1) % 2)
                yield
                for kc in range(8):
                    S.add("pe", lambda e, kc=kc: e.matmul(bank(b2)[:, 0:128], lhsT=xnT[:, kc, tcols], rhs=Wvh[0][:, kc, :], start=(kc == 0), stop=(kc == 7)),
                          reads=[("xnT", t), ("Wvh", 0)], writes=[bk(b2)])
                yield
                S.add("dve", lambda e: e.tensor_copy(out=Vhb[0][:, t, 0:128], in_=bank(b2)[:, 0:128]), reads=[bk(b2)], writes=[("Vh", 0, t)])
                if not own:
                    return
                for j in range(2):
                    for kc in range(8):
                        S.add("pe", lambda e, kc=kc, j=j: e.matmul(bank(b0 + j), lhsT=xnT[:, kc, tcols], rhs=Wuv[:, kc, j * 512:(j + 1) * 512], start=(kc == 0), stop=(kc == 7)),
                              reads=[("xnT", t), "Wuv"], writes=[bk(b0 + j)])
                yield
                S.add("act", lambda e: e.activation(out=ug[sl][:], in_=bank(b0), func=AF.Gelu_apprx_tanh), reads=[bk(b0)], writes=[("ug", sl)])
                S.add("act", lambda e: e.activation(out=vg[sl][:], in_=bank(b1), func=AF.Gelu_apprx_tanh, accum_out=lnst[sl][:, 0:1]),
                      reads=[bk(b1)], writes=[("vg", sl), ("lnst", sl, 0)])
                yield
                for j in range(2):
                    for kc in range(8):
                        S.add("pe", lambda e, kc=kc, j=j: e.matmul(bank(b2 + j), lhsT=xnT[:, kc, tcols], rhs=Wgt[:, kc, j * 512:(j + 1) * 512], start=(kc == 0), stop=False),
                              reads=[("xnT", t), ("Wgt", 0)], writes=[bk(b2 + j)])
                    S.add("pe", lambda e, j=j: e.matmul(bank(b2 + j), lhsT=onesrow[0:1, 0:128], rhs=bgate_row[0:1, j * 512:(j + 1) * 512], start=False, stop=True),
                          reads=["onesrow", "bgate_row"], writes=[bk(b2 + j)])
                S.add("dve", lambda e: e.tensor_scalar(out=lnst[sl][:, 1:2], in0=lnst[sl][:, 0:1], scalar1=-1.0 / 512, scalar2=None, op0=ALU.mult),
                      reads=[("lnst", sl, 0)], writes=[("lnst", sl, 1)])
                yield
                S.add("act", lambda e: e.activation(out=vln[sl][:], in_=vg[sl][:], func=AF.Square, bias=lnst[sl][:, 1:2], scale=1.0, accum_out=lnst[sl][:, 2:3]),
                      reads=[("vg", sl), ("lnst", sl, 1)], writes=[("vln", sl), ("lnst", sl, 2)])
                S.add("act", lambda e: e.activation(out=lnst[sl][:, 2:3], in_=lnst[sl][:, 2:3], func=AF.Sqrt, bias=epsb[:, 0:1], scale=1.0 / 512),
                      reads=[("lnst", sl, 2), "epsb"], writes=[("lnst", sl, 2)])
                yield
                S.add("dve", lambda e: e.reciprocal(out=lnst[sl][:, 2:3], in_=lnst[sl][:, 2:3]), reads=[("lnst", sl, 2)], writes=[("lnst", sl, 2)])
                S.add("dve", lambda e: e.tensor_scalar(out=vg[sl][:], in0=vg[sl][:], scalar1=lnst[sl][:, 1:2], scalar2=lnst[sl][:, 2:3], op0=ALU.add, op1=ALU.mult),
                      reads=[("vg", sl), ("lnst", sl, 1), ("lnst", sl, 2)], writes=[("vg", sl)])
                S.add("dve", lambda e: e.tensor_tensor(out=vg[sl][:], in0=vg[sl][:], in1=lng_b[:], op=ALU.mult), reads=[("vg", sl), "lng_b"], writes=[("vg", sl)])
                S.add("dve", lambda e: e.tensor_tensor(out=vln[sl][:], in0=vg[sl][:], in1=lnb_b[:], op=ALU.add), reads=[("vg", sl), "lnb_b"], writes=[("vln", sl)])
                S.add("act", lambda e: e.activation(out=gsg[sl][:], in_=P23[:].rearrange("p a n -> p (a n)"), func=AF.Sigmoid),
                      reads=[bk(b2), bk(b3)], writes=[("gsg", sl)])
                yield
                S.add("pe", lambda e: e.matmul(bank(b0), lhsT=bs8, rhs=e8[:], start=True, stop=False, skip_group_check=True),
                      reads=["bs8", "e8"], writes=[bk(b0)])
                for g in range(8):
                    S.add("pe", lambda e, g=g: e.matmul(bank(b0)[:, g * 64:(g + 1) * 64], lhsT=WsT[:, g, :], rhs=vln[sl][:, g * 64:(g + 1) * 64], start=False, stop=(g == 7), skip_group_check=True),
                          reads=["WsT", ("vln", sl)], writes=[bk(b0)])
                for j in range(2):
                    for kc in range(8):
                        S.add("pe", lambda e, kc=kc, j=j: e.matmul(bank(b2 + j), lhsT=xnT[:, kc, tcols], rhs=Wgt[:, kc, 1024 + j * 512:1024 + (j + 1) * 512], start=(kc == 0), stop=False),
                              reads=[("xnT", t), ("Wgt", 1)], writes=[bk(b2 + j)])
                    S.add("pe", lambda e, j=j: e.matmul(bank(b2 + j), lhsT=onesrow[0:1, 0:128], rhs=bgate_row[0:1, 1024 + j * 512:1024 + (j + 1) * 512], start=False, stop=True),
                          reads=["onesrow", "bgate_row"], writes=[bk(b2 + j)])
                yield
                S.add("dve", lambda e: e.tensor_tensor(out=osg[sl][:], in0=bank(b0), in1=ug[sl][:], op=ALU.mult), reads=[bk(b0), ("ug", sl)], writes=[("osg", sl)])
                S.add("act", lambda e: e.activation(out=gda_t[sl][:], in_=P23[:].rearrange("p a n -> p (a n)"), func=AF.Sigmoid),
                      reads=[bk(b2), bk(b3)], writes=[("gda_t", sl)])
                S.add("pool", lambda e: e.dma_start(out=GDA[t * 128:(t + 1) * 128, :], in_=gda_t[sl][:]),
                      reads=[("gda_t", sl)], writes=[("GDA", t)], dma_sem=("gda_st", sl))
                yield
                for kc in range(4):
                    S.add("pe", lambda e, kc=kc: e.matmul(bank(b1)[:, kc * 128:(kc + 1) * 128], lhsT=osg[sl][:, kc * 128:(kc + 1) * 128], rhs=ident[:], start=True, stop=True),
                          reads=[("osg", sl), "ident"], writes=[bk(b1)])
                yield
                S.add("act", lambda e: e.copy(out=osgT[sl][:], in_=bank(b1).rearrange("p (k t) -> p k t", k=4)), reads=[bk(b1)], writes=[("osgT", sl)])
                yield
                for j in range(2):
                    for kc in range(4):
                        S.add("pe", lambda e, kc=kc, j=j: e.matmul(bank(b2 + j), lhsT=osgT[sl][:, kc, :], rhs=PA[:, kc, j * 512:(j + 1) * 512], start=(kc == 0), stop=(kc == 3)),
                              reads=[("osgT", sl), "PA"], writes=[bk(b2 + j)])
                yield
                S.add("dve", lambda e: e.tensor_tensor(out=zsg_t[sl][:], in0=P23[:].rearrange("p a n -> p (a n)"), in1=gsg[sl][:], op=ALU.mult),
                      reads=[bk(b2), bk(b3), ("gsg", sl)], writes=[("zsg_t", sl)])
                S.add("pool", lambda e: e.dma_start(out=ZSG[t * 128:(t + 1) * 128, :], in_=zsg_t[sl][:]),
                      reads=[("zsg_t", sl)], writes=[("ZSG", t)], dma_sem=("zsg_st", sl))

            order = list(range(NT_OWN, NT_ALL)) + list(range(NT_OWN))
            slot_tiles = [order[0::2], order[1::2]]

            def slot_gen(sl):
                tiles = slot_tiles[sl]
                for i, t in enumerate(tiles):
                    yield from tile_gen(t, sl, i, tiles)

            for sl in range(2):
                front_load(slot_tiles[sl][0], sl, 0)
                front_load(slot_tiles[sl][1], sl, 1)
            for sl in range(2):
                front_compute(slot_tiles[sl][0], sl, 0)
            active = [slot_gen(0), slot_gen(1)]
            step = 0
            STAGGER = 5
            while any(a is not None for a in active):
                for sl in range(2):
                    if active[sl] is not None and (sl == 0 or step >= STAGGER):
                        try:
                            next(active[sl])
                        except StopIteration:
                            active[sl] = None
                step += 1
            S.barrier()

        if "B" in phases:
            kT = wa(0, 8192).rearrange("p (c t) -> p c t", c=2)
            qTL = wa(8192, 4096).rearrange("p (c t) -> p c t", c=2)
            qTR = wa(12288, 4096).rearrange("p (c t) -> p c t", c=2)
            PT = [wa(32768 + i * 1024, 1024).rearrange("p (c n) -> p c n", c=2) for i in range(3)]
            rk = [TA[:, i * 512:(i + 1) * 512] for i in range(2)]
            o4 = TA[:, 1024:1536].rearrange("p (q d) -> p q d", q=4)
            sq = [TAb[:, 3072 + i * 512:3072 + (i + 1) * 512] for i in range(2)]
            accs = TA[:, 2048:3488].rearrange("p (s d) -> p s d", s=9)
            rec9 = sbt("rec9", [128, 8], F32)
            ssall = sbt("ssall", [128, 16, 8], F32)

            ld(ones64[:], c_ones64[:, :], "ones64")
            ld(diag[:], c_diag[:, :], "diag")
            ld(gq_b[:], q_norm_g[0:1, :].broadcast_to([128, 64]), "gq_b")
            ld(gk_b[:], k_norm_g[0:1, :].broadcast_to([128, 64]), "gk_b")
            for i in range(2):
                ld(gqT[64 * i:64 * i + 64, :], q_norm_g[0].rearrange("(d o) -> d o", o=1), ("gqT", i), allow_slow_non_contiguous=True)
                ld(gkT[64 * i:64 * i + 64, :], k_norm_g[0].rearrange("(d o) -> d o", o=1), ("gkT", i), allow_slow_non_contiguous=True)
            for i, lv in enumerate((lam_q1, lam_k1, lam_q2, lam_k2)):
                ld(lam4[:, i, :], lv[0:1, :].broadcast_to([128, 64]), ("lam4", i))
            ld(sublng[:], subln_g[0:1, :].broadcast_to([128, 128]), "sublng")
            for h in range(H):
                S.add("dve", lambda e, h=h: e.tensor_scalar(out=gqs[:, h:h + 1], in0=gqT[:], scalar1=float(2.0 ** (h - 2)), scalar2=None, op0=ALU.mult),
                      reads=[("gqT", 0), ("gqT", 1)], writes=[("gqs", h)])
            S.add("dve", lambda e: e.tensor_tensor(out=lamt[:, 0, :], in0=lam4[:, 0, :], in1=lam4[:, 1, :], op=ALU.mult),
                  reads=[("lam4", 0), ("lam4", 1)], writes=[("lamt", 0)])
            S.add("dve", lambda e: e.tensor_tensor(out=lamt[:, 1, :], in0=lam4[:, 2, :], in1=lam4[:, 3, :], op=ALU.mult),
                  reads=[("lam4", 2), ("lam4", 3)], writes=[("lamt", 1)])
            S.add("dve", lambda e: e.reduce_sum(out=lams[:], in_=lamt[:], axis=mybir.AxisListType.X),
                  reads=[("lamt", 0), ("lamt", 1)], writes=["lams"])
            S.add("act", lambda e: e.activation(out=lams[:], in_=lams[:], func=AF.Exp), reads=["lams"], writes=["lams"])
            S.add("dve", lambda e: e.tensor_tensor(out=neglam[:], in0=lams[:, 1:2], in1=lams[:, 0:1], op=ALU.subtract),
                  reads=["lams"], writes=["neglam"])
            S.add("dve", lambda e: e.tensor_scalar(out=neglam[:], in0=neglam[:], scalar1=-float(LAM_INIT), scalar2=None, op0=ALU.add),
                  reads=["neglam"], writes=["neglam"])
            S.add("dve", lambda e: e.reduce_max(out=cmax[:, 0:1], in_=gq_b[:], axis=mybir.AxisListType.X, apply_absolute_value=True),
                  reads=["gq_b"], writes=[("cmax", 0)])
            S.add("dve", lambda e: e.reduce_max(out=cmax[:, 1:2], in_=gk_b[:], axis=mybir.AxisListType.X, apply_absolute_value=True),
                  reads=["gk_b"], writes=[("cmax", 1)])
            S.add("dve", lambda e: e.scalar_tensor_tensor(out=negc[:], in0=cmax[:, 0:1], scalar=-8.0, in1=cmax[:, 1:2], op0=ALU.mult, op1=ALU.mult),
                  reads=[("cmax", 0), ("cmax", 1)], writes=["negc"])
            S.add("dve", lambda e: e.tensor_scalar(out=sublng[:], in0=sublng[:], scalar1=float(1.0 - LAM_INIT), scalar2=None, op0=ALU.mult),
                  reads=["sublng"], writes=["sublng"])

            for (tl, nm) in ((kT, "kaug"), (qTL, "qaugL"), (qTR, "qaugR")):
                S.add("pool", lambda e, tl=tl: e.memset(tl[64:96, 0, :], 0.0), writes=[(nm, 0)])
                S.add("pool", lambda e, tl=tl: e.memset(tl[0:64, 1, :], 0.0), writes=[(nm, 1)])
            for c in range(2):
                r0 = 64 if c == 0 else 0
                ld(kT[r0:r0 + 4, c, :], c_kaug[:, :], ("kaug", c))
                ld(qTL[r0:r0 + 4, c, :], c_qaugL[:, :], ("qaugL", c))
                ld(qTR[r0:r0 + 4, c, :], c_qaugR[:, :], ("qaugR", c))
            for i in (1, 2):
                S.add("pool", lambda e, i=i: e.memset(Vhb[i][:, :, 128:130], 1.0), writes=[("Vones", i)])

            def fill_heads(h):
                return [j for j in (h + 1, h + 2) if j < H] if h % 2 == 0 else []

            def load_wvp(h):
                hs = fill_heads(h)
                if hs:
                    n = 128 * len(hs)
                    ld(Wvp[:, :, 0:n], w_in_v[:, :, 3072 + hs[0] * 128:3072 + hs[0] * 128 + n], "Wvp", eng="pool")

            def vfill_items(h):
                hs = fill_heads(h)
                nh = len(hs)
                items = []
                if nh == 0:
                    return items
                for g in range(16):
                    for q in range(2):
                        kb = 2 * g + q
                        reg = bank(7)[:, q * 128 * nh:(q + 1) * 128 * nh]
                        for kc in range(8):
                            def mm(kb=kb, kc=kc, reg=reg, q=q, g=g):
                                S.add("pe", lambda e: e.matmul(reg, lhsT=xnT[:, kc, kb * 128:(kb + 1) * 128], rhs=Wvp[:, kc, 0:128 * nh], start=(kc == 0), stop=(kc == 7), skip_group_check=True),
                                      reads=[("xnT", kb), "Wvp"], writes=[bk(7)])
                                if q == 1 and kc == 7:
                                    src = bank(7)[:, 0:256 * nh].rearrange("p (q j n) -> p q j n", q=2, j=nh)
                                    for j, hj in enumerate(hs):
                                        S.add("dve", lambda e, j=j, hj=hj: e.tensor_copy(out=Vhb[hj % 3][:, 2 * g:2 * g + 2, 0:128], in_=src[:, :, j, :]),
                                              reads=[bk(7)], writes=[("Vh", hj % 3, 2 * g), ("Vh", hj % 3, 2 * g + 1)])
                            items.append(mm)
                return items

            load_wvp(0)

            accb = [4, 5, 6]

            def acc(c, qi):
                idx = c * 4 + qi
                return bank(accb[idx // 3])[:, (idx % 3) * 160:(idx % 3) * 160 + 129]

            def acck(c, qi):
                return bk(accb[(c * 4 + qi) // 3])


            def do_head(h):
                wb = h % 2
                Vh = Vhb[h % 3]
                vfill = vfill_items(h)
                if h % 2 == 1 and h + 1 < H:
                    load_wvp(h + 1)
                groups = [(0, tg) for tg in range(8)] + [(1, tg) for tg in range(4)]

                def proj_pe(s_):
                    isq, tg = groups[s_]
                    pK = bank(s_ % 4)
                    col0 = 128 if isq == 0 else 0
                    wkey = ("wk", wb) if isq == 0 else ("wq", wb)
                    for kc in range(8):
                        S.add("pe", lambda e, kc=kc: e.matmul(pK, lhsT=wqk[wb][:, kc, col0:col0 + 128], rhs=xnT[:, kc, tg * 512:(tg + 1) * 512], start=(kc == 0), stop=(kc == 7)),
                              reads=[wkey] + [("xnT", 4 * tg + i) for i in range(4)], writes=[bk(s_ % 4)])
                    S.add("act", lambda e: e.activation(out=sq[s_ % 2][:], in_=pK, func=AF.Square),
                          reads=[bk(s_ % 4)], writes=[("sq", s_ % 2)])

                def proj_post(s_):
                    isq, tg = groups[s_]
                    pK = bank(s_ % 4)
                    pSS = bank(4 + s_ % 2)
                    pb = s_ % 2
                    tcols = slice(tg * 512, (tg + 1) * 512)
                    S.add("pe", lambda e: e.matmul(pSS, lhsT=ones64[:], rhs=sq[pb][:], start=True, stop=True),
                          reads=["ones64", ("sq", pb)], writes=[bk(4 + pb)])
                    S.add("act", lambda e: e.activation(out=rk[pb][:], in_=pSS, func=AF.Ln, bias=epsb[:, 0:1], scale=1.0),
                          reads=[bk(4 + pb), "epsb"], writes=[("rk", pb)])
                    S.add("act", lambda e: e.activation(out=rk[pb][:], in_=rk[pb][:], func=AF.Exp, scale=-0.5),
                          reads=[("rk", pb)], writes=[("rk", pb)])
                    for c in range(2):
                        rows = slice(64 * c, 64 * c + 64)
                        if isq == 0:
                            S.add("dve", lambda e, c=c, rows=rows: e.scalar_tensor_tensor(out=kT[rows, c, tcols], in0=pK[rows, :], scalar=gkT[rows, 0:1], in1=rk[pb][rows, :], op0=ALU.mult, op1=ALU.mult),
                                  reads=[bk(s_ % 4), ("rk", pb), ("gkT", c)], writes=[("kT", c, tg)])
                        else:
                            S.add("dve", lambda e, c=c, rows=rows: e.scalar_tensor_tensor(out=qTL[rows, c, tcols], in0=pK[rows, :], scalar=gqs[rows, h:h + 1], in1=rk[pb][rows, :], op0=ALU.mult, op1=ALU.mult),
                                  reads=[bk(s_ % 4), ("rk", pb), ("gqs", h)], writes=[("qTL", c, tg)])
                            S.add("pool", lambda e, c=c, rows=rows: e.tensor_copy(out=qTR[rows, c, tcols], in_=qTL[rows, c, tcols]),
                                  reads=[("qTL", c, tg)], writes=[("qTR", c, tg)])

                for s_ in range(len(groups) + 1):
                    if s_ < len(groups):
                        proj_pe(s_)
                    if s_ >= 1:
                        proj_post(s_ - 1)
                if h + 1 < H:
                    load_wqk(h + 1)

                slope = float(2.0 ** (-(h + 1)))
                iters = [(G, kb) for G in range(4) for kb in range(32)]

                def emit_qk(it):
                    G, kb = iters[it]
                    sb = it % 2
                    pS = PSP[sb]
                    late = []
                    for c in range(2):
                        kkey = ("kT", c, kb // 4)
                        kaugk = ("kaug", c)
                        kcols = slice(kb * 128, (kb + 1) * 128)

                        rall = slice(0, 96) if c == 0 else slice(0, 128)
                        rdat = slice(0, 64) if c == 0 else slice(64, 128)

                        def mmL(q0, q1, c=c, kcols=kcols, rall=rall):
                            return lambda e: e.matmul(pS[:, c, q0 * 128:q1 * 128], lhsT=kT[rall, c, kcols], rhs=qTL[rall, c, (4 * G + q0) * 128:(4 * G + q1) * 128], start=True, stop=True)

                        def mmR(q0, q1, c=c, kcols=kcols, rall=rall):
                            return lambda e: e.matmul(pS[:, c, q0 * 128:q1 * 128], lhsT=kT[rall, c, kcols], rhs=qTR[rall, c, (4 * G + q0) * 128:(4 * G + q1) * 128], start=True, stop=True)

                        rdL = [kkey, kaugk, ("qTL", c, G), ("qaugL", c)]
                        rdR = [kkey, kaugk, ("qTR", c, G), ("qaugR", c)]
                        wr = [bk(2 * sb + c)]
                        if kb >= 16 or kb < 4 * G:
                            S.add("pe", mmL(0, 4), reads=rdL, writes=wr)
                        elif kb > 4 * G + 3:
                            S.add("pe", mmR(0, 4), reads=rdR, writes=wr)
                        else:
                            d = kb - 4 * G
                            if d > 0:
                                S.add("pe", mmR(0, d), reads=rdR, writes=wr)
                            S.add("pe", lambda e, c=c, kcols=kcols, d=d, rall=rall: e.matmul(pS[:, c, d * 128:512], lhsT=kT[rall, c, kcols], rhs=qTL[rall, c, (4 * G + d) * 128:(4 * G + 4) * 128], start=True, stop=False, skip_group_check=True),
                                  reads=rdL, writes=wr)
                            late.append((lambda e, c=c, d=d: e.matmul(pS[:, c, d * 128:(d + 1) * 128], lhsT=ident[:], rhs=diag[:], start=False, stop=True, skip_group_check=True), wr))

                    for fn_, wr_ in late:
                        S.add("pe", fn_, reads=["ident", "diag"], writes=wr_)

                def emit_exp_av(it):
                    G, kb = iters[it]
                    sb = it % 2
                    pt = it % 3
                    pS = PSP[sb]
                    S.add("act", lambda e: e.activation(out=PT[pt][:], in_=pS[:], func=AF.Exp, bias=negc[:, 0:1], scale=slope),
                          reads=[bk(2 * sb), bk(2 * sb + 1), "negc"], writes=[("PT", pt)])
                    for c in range(2):
                        for qi in range(4):
                            S.add("pe", lambda e, c=c, qi=qi: e.matmul(acc(c, qi), lhsT=PT[pt][:, c, qi * 128:(qi + 1) * 128], rhs=Vh[:, kb, 0:129], start=(kb == 0 and (c * 4 + qi) % 3 == 0), stop=(kb == 31), skip_group_check=True),
                                  reads=[("PT", pt), ("Vh", h % 3, kb), ("Vones", h % 3)], writes=[acck(c, qi)])

                def emit_epilogue(G):
                    for b3 in range(3):
                        ns = 3 if b3 < 2 else 2
                        S.add("dve", lambda e, b3=b3, ns=ns: e.tensor_copy(out=accs[:, 3 * b3:3 * b3 + ns, 0:129], in_=bank(accb[b3])[:, 0:160 * ns].rearrange("p (s d) -> p s d", s=ns)[:, :, 0:129]),
                              reads=[bk(accb[b3])], writes=[("accs", b3)])
                    ak = [("accs", 0), ("accs", 1), ("accs", 2)]
                    S.add("dve", lambda e: e.reciprocal(out=rec9[:].unsqueeze(2), in_=accs[:, 0:8, 128:129]), reads=ak, writes=["rec9"])
                    S.add("dve", lambda e: e.tensor_scalar(out=rec9[:, 4:8], in0=rec9[:, 4:8], scalar1=neglam[:, 0:1], scalar2=None, op0=ALU.mult),
                          reads=["rec9", "neglam"], writes=["rec9"])
                    S.add("dve", lambda e: e.tensor_tensor(out=accs[:, 0:8, 0:128], in0=accs[:, 0:8, 0:128], in1=rec9[:].unsqueeze(2).broadcast_to([128, 8, 128]), op=ALU.mult),
                          reads=ak + ["rec9"], writes=ak)
                    S.add("dve", lambda e: e.tensor_tensor(out=o4[:], in0=accs[:, 0:4, 0:128], in1=accs[:, 4:8, 0:128], op=ALU.add), reads=ak, writes=["o4"])
                    S.add("dve", lambda e: e.tensor_tensor(out=accs[:, 0:4, 0:128], in0=o4[:], in1=o4[:], op=ALU.mult), reads=["o4"], writes=ak)
                    S.add("dve", lambda e: e.reduce_sum(out=ssall[:, 4 * G:4 * G + 4, h], in_=accs[:, 0:4, 0:128], axis=mybir.AxisListType.X),
                          reads=ak, writes=[("ssall", h, G)])
                    S.add("pool", lambda e: e.tensor_copy(out=oda[:, 4 * G:4 * G + 4, h * 128:(h + 1) * 128], in_=o4[:]),
                          reads=["o4"], writes=[("B32", 4 * G + q) for q in range(4)])

                emit_qk(0)
                for it in range(len(iters)):
                    if it + 1 < len(iters):
                        emit_qk(it + 1)
                    for _ in range(2):
                        if vfill:
                            vfill.pop(0)()
                    emit_exp_av(it)
                    if iters[it][1] == 31:
                        emit_epilogue(iters[it][0])

            for h_ in range(H):
                do_head(h_)

            allss = [("ssall", h, G) for h in range(H) for G in range(4)]
            allb = [("B32", t) for t in range(NT_OWN)]
            S.add("act", lambda e: e.activation(out=ssall[:], in_=ssall[:], func=AF.Sqrt, bias=epsb[:, 0:1], scale=1.0 / 128),
                  reads=allss + ["epsb"], writes=allss)
            S.add("dve", lambda e: e.reciprocal(out=ssall[:], in_=ssall[:]), reads=allss, writes=allss + ["ssall_r"])
            if dbg:
                for t in range(NT_OWN):
                    S.add("sp", lambda e, t=t: e.dma_start(out=ODA[t * 128:(t + 1) * 128, :], in_=oda[:, t, :]), reads=[("B32", t)], dma_sem=("oda_dbg", t), final=True)
            S.barrier()

        if "C" in phases:
            PB = wa(0, 8192).rearrange("p (k n) -> p k n", k=8)
            Wout = wa(8192, 8192).rearrange("p (k n) -> p k n", k=8)
            RB = [18432, 0]

            def ffn_views(i):
                base = RB[i % 2]
                nf = FBLOCKS[i][1]
                wg = wa(base, 6144).rearrange("p (k n) -> p k n", k=8)
                wu = wa(base + 6144, 6144).rearrange("p (k n) -> p k n", k=8)
                wd = wa(base + 12288, 6144).rearrange("p (f n) -> p f n", f=6)
                return wg, wu, wd, nf

            def rpart(i, part):
                return ("Rp", 1 if RB[i % 2] else 0, part)

            aT = TAb[:, 2048:5120].rearrange("p (f n) -> p f n", f=6)
            ld(PB, w_proj_da.rearrange("(kc p) n -> p kc n", p=128), "PB", eng="pool")
            ld(Wout, w_out.rearrange("(kc p) n -> p kc n", p=128), "Wout", eng="pool")

            def load_ffn(i):
                wg, wu, wd, nf = ffn_views(i)
                f0 = FBLOCKS[i][0]
                extra = ["PB", "Wout"] if RB[i % 2] == 0 else []
                r = 1 if RB[i % 2] else 0
                S.add("pool", lambda e: e.dma_start(out=wg[:, :, 0:nf * 128], in_=w_ffn_gate.rearrange("(kc p) n -> p kc n", p=128)[:, :, f0 * 128:(f0 + nf) * 128]),
                      writes=[rpart(i, 0)] + extra, dma_sem=("ffn", r, 0))
                S.add("pool", lambda e: e.dma_start(out=wu[:, :, 0:nf * 128], in_=w_ffn_up.rearrange("(kc p) n -> p kc n", p=128)[:, :, f0 * 128:(f0 + nf) * 128]),
                      writes=[rpart(i, 1)] + extra, dma_sem=("ffn", r, 1))
                S.add("pool", lambda e: e.dma_start(out=wd[:, 0:nf, :], in_=w_ffn_down.rearrange("(f p) n -> p f n", p=128)[:, f0:f0 + nf, :]),
                      writes=[rpart(i, 2)] + extra, dma_sem=("ffn", r, 2))

            load_ffn(0)
            zf = TA[:, 0:1024]
            odaT = TAb[:, 2048:3072].rearrange("p (k t) -> p k t", k=8)
            zl = [TAb[:, 3072:4096]] * 2
            gl = [TAb[:, 4096:5120]] * 2
            zb = TAb[:, 5120:6144]
            zT = TAb[:, 6144:7168].rearrange("p (k t) -> p k t", k=8)
            sig = [TA[:, i * 512:(i + 1) * 512] for i in range(2)]

            def c_a0(t):
                ld(hacc[:, t, :], x[t * 128:(t + 1) * 128, :], ("hacc", t))
                odat = oda[:, t, :].rearrange("p (h d) -> p h d", h=8)
                S.add("dve", lambda e: e.tensor_tensor(out=odat, in0=odat, in1=ssall[:, t, :].unsqueeze(2).broadcast_to([128, 8, 128]), op=ALU.mult),
                      reads=[("B32", t), "ssall_r"], writes=[("B32", t)])
                S.add("dve", lambda e: e.tensor_tensor(out=odat, in0=odat, in1=sublng[:].unsqueeze(1).broadcast_to([128, 8, 128]), op=ALU.mult),
                      reads=[("B32", t), "sublng"], writes=[("B32", t)])

            def c_a1(t):
                ld(zl[0], ZSG[t * 128:(t + 1) * 128, :], ("zl", 0))
                ld(gl[0], GDA[t * 128:(t + 1) * 128, :], ("gl", 0))
                for hh in range(8):
                    S.add("pe", lambda e, hh=hh: e.matmul(bank(4 + hh // 4)[:, (hh % 4) * 128:(hh % 4 + 1) * 128], lhsT=oda[:, t, hh * 128:(hh + 1) * 128], rhs=ident[:], start=True, stop=True),
                          reads=[("B32", t), "ident"], writes=[bk(4 + hh // 4)])
                S.add("act", lambda e: e.copy(out=odaT[:], in_=PSP[2][:].rearrange("p a (b t) -> p (a b) t", t=128)), reads=[bk(4), bk(5)], writes=["odaT"])

            def c_a2(t):
                for j in range(2):
                    for hh in range(8):
                        S.add("pe", lambda e, hh=hh, j=j: e.matmul(bank(2 + j), lhsT=odaT[:, hh, :], rhs=PB[:, hh, j * 512:(j + 1) * 512], start=(hh == 0), stop=(hh == 7)),
                              reads=["odaT", "PB"], writes=[bk(2 + j)])
                S.add("dve", lambda e: e.tensor_tensor(out=zf[:], in0=PSP[1][:].rearrange("p a n -> p (a n)"), in1=gl[0], op=ALU.mult),
                      reads=[bk(2), bk(3), ("gl", 0)], writes=["zf"])
                S.add("dve", lambda e: e.tensor_tensor(out=zb[:], in0=zf[:], in1=zl[0], op=ALU.add), reads=["zf", ("zl", 0)], writes=["zb"])

            def c_a3(t):
                for kc in range(8):
                    S.add("pe", lambda e, kc=kc: e.matmul(bank(kc // 4)[:, (kc % 4) * 128:(kc % 4 + 1) * 128], lhsT=zb[:, kc * 128:(kc + 1) * 128], rhs=ident[:], start=True, stop=True),
                          reads=["zb", "ident"], writes=[bk(kc // 4)])
                S.add("act", lambda e: e.copy(out=zT[:], in_=PSP[0][:].rearrange("p a (b t) -> p (a b) t", t=128)), reads=[bk(0), bk(1)], writes=["zT"])

            def c_a4(t):
                for j in range(2):
                    for kc in range(8):
                        S.add("pe", lambda e, kc=kc, j=j: e.matmul(bank(6 + j), lhsT=zT[:, kc, :], rhs=Wout[:, kc, j * 512:(j + 1) * 512], start=(kc == 0), stop=(kc == 7)),
                              reads=["zT", "Wout"], writes=[bk(6 + j)])
                S.add("dve", lambda e: e.tensor_tensor(out=hacc[:, t, :], in0=PSP[3][:].rearrange("p a n -> p (a n)"), in1=hacc[:, t, :], op=ALU.add),
                      reads=[bk(6), bk(7), ("hacc", t)], writes=[("hacc", t)])

            def c_b1(t):
                rms_stage1(hacc[:, t, :], ("hacc", t), t % 2)

            def c_b2(t):
                rms_stage2(g2T, "g2T", hnT[:, t, :, :], ("B32", t), t % 2)

            stages = [c_a0, c_a1, c_a2, c_a3, c_a4, c_b1, c_b2]
            for s_ in range(NT_OWN + len(stages) - 1):
                for k in reversed(range(len(stages))):
                    t = s_ - k
                    if 0 <= t < NT_OWN:
                        stages[k](t)

            for i in range(len(FBLOCKS)):
                wg, wu, wd, nf = ffn_views(i)
                if i + 1 < len(FBLOCKS):
                    load_ffn(i + 1)
                for G in range(4):
                    for f in range(nf):
                        pb = f % 2
                        for (wmat, wkey, bi) in ((wg, rpart(i, 0), 0), (wu, rpart(i, 1), 1)):
                            for kc in range(8):
                                S.add("pe", lambda e, kc=kc, f=f, wmat=wmat, pb=pb, bi=bi, G=G: e.matmul(bank(2 * pb + bi), lhsT=wmat[:, kc, f * 128:(f + 1) * 128], rhs=hnT[:, 4 * G:4 * G + 4, kc, :], start=(kc == 0), stop=(kc == 7)),
                                      reads=[wkey] + [("B32", 4 * G + q) for q in range(4)], writes=[bk(2 * pb + bi)])
                        S.add("act", lambda e, pb=pb: e.activation(out=sig[pb][:], in_=bank(2 * pb), func=AF.Sigmoid), reads=[bk(2 * pb)], writes=[("sig", pb), "zf"])
                        S.add("dve", lambda e, pb=pb: e.tensor_tensor(out=sig[pb][:], in0=bank(2 * pb), in1=sig[pb][:], op=ALU.mult),
                              reads=[bk(2 * pb), ("sig", pb)], writes=[("sig", pb)])
                        S.add("dve", lambda e, pb=pb, f=f: e.tensor_tensor(out=aT[:, f, :], in0=bank(2 * pb + 1), in1=sig[pb][:], op=ALU.mult),
                              reads=[bk(2 * pb + 1), ("sig", pb)], writes=[("aT", f), "odaT", ("zl", 0), ("gl", 0)])
                    for ti in range(4):
                        t = 4 * G + ti
                        db = ti % 2
                        for j in range(2):
                            for f in range(nf):
                                S.add("pe", lambda e, f=f, j=j, ti=ti, db=db, wd=wd, nf=nf: e.matmul(bank(4 + 2 * db + j), lhsT=aT[:, f, ti * 128:(ti + 1) * 128], rhs=wd[:, f, j * 512:(j + 1) * 512], start=(f == 0), stop=(f == nf - 1)),
                                      reads=[("aT", f), rpart(i, 2)], writes=[bk(4 + 2 * db + j)])
                        S.add("dve", lambda e, t=t, db=db: e.tensor_tensor(out=hacc[:, t, :], in0=PSP[2 + db][:].rearrange("p a n -> p (a n)"), in1=hacc[:, t, :], op=ALU.add),
                              reads=[bk(4 + 2 * db), bk(5 + 2 * db), ("hacc", t)], writes=[("hacc", t)])
                        if i == len(FBLOCKS) - 1:
                            S.add("sp", lambda e, t=t: e.dma_start(out=out[t * 128:(t + 1) * 128, :], in_=hacc[:, t, :]),
                                  reads=[("hacc", t)], dma_sem=("out", t), final=True)

        if dbg and "A" in phases:
            pass
        nsem = S.emit(nc, st)
    return nc


def _bf16(a):
    return np.asarray(a, dtype=np.float32).astype(ml_dtypes.bfloat16)


def make_consts(half):
    ident = _bf16(np.eye(128))
    ones64 = np.zeros((128, 128), np.float32)
    ones64[:64, :64] = 1.0 / 64
    ones64[64:, 64:] = 1.0 / 64
    ones64 = _bf16(ones64)
    onesrow = _bf16(np.ones((1, 512)))
    ii = np.arange(128)
    diag = _bf16(-2.0 * np.maximum(ii[:, None] - ii[None, :], 0))
    own_blocks = half * 16 + np.arange(16)
    oth_blocks = (1 - half) * 16 + np.arange(16)
    kblocks = np.concatenate([own_blocks, oth_blocks])
    sign = np.concatenate([np.ones(16), np.full(16, 1.0 if half == 1 else -1.0)])
    kaug = np.zeros((4, S_ALL), np.float32)
    for kb in range(32):
        sl = slice(kb * 128, (kb + 1) * 128)
        kaug[0, sl] = sign[kb] * 128.0 * kblocks[kb]
        kaug[1, sl] = sign[kb] * ii
        kaug[2, sl] = sign[kb]
        kaug[3, sl] = sign[kb]
    qL = np.zeros((4, S_OWN), np.float32)
    for qb in range(16):
        sl = slice(qb * 128, (qb + 1) * 128)
        qL[0, sl] = 1.0
        qL[1, sl] = 1.0
        qL[2, sl] = -128.0 * own_blocks[qb]
        qL[3, sl] = -ii
    e8 = np.zeros((8, 512), np.float32)
    for g in range(8):
        e8[g, g * 64:(g + 1) * 64] = 1.0
    return {"c_ident": ident, "c_ones64": ones64, "c_onesrow": onesrow, "c_diag": diag, "c_e8": _bf16(e8),
            "c_kaug": _bf16(kaug), "c_qaugL": _bf16(qL), "c_qaugR": _bf16(-qL)}


def make_in_maps(inputs):
    f = lambda a: np.ascontiguousarray(np.asarray(a, dtype=np.float32))
    x = f(inputs["x"])
    shared = {
        "norm1_g": f(inputs["norm1_g"]).reshape(1, D),
        "w_in": f(inputs["w_in"])[0],
        "b_gate": f(inputs["b_gate"]).reshape(1, 2048),
        "sg_ln_g": f(inputs["sg_ln_g"]).reshape(1, 512),
        "sg_ln_b": f(inputs["sg_ln_b"]).reshape(1, 512),
        "sg_wT": np.ascontiguousarray(f(inputs["sg_w"])[0].transpose(2, 0, 1)),
        "sg_b": f(inputs["sg_b"]).reshape(8, 128),
        "q_norm_g": f(inputs["q_norm_g"]).reshape(1, 64),
        "k_norm_g": f(inputs["k_norm_g"]).reshape(1, 64),
        "lam_q1": f(inputs["lam_q1"]).reshape(1, 64),
        "lam_k1": f(inputs["lam_k1"]).reshape(1, 64),
        "lam_q2": f(inputs["lam_q2"]).reshape(1, 64),
        "lam_k2": f(inputs["lam_k2"]).reshape(1, 64),
        "subln_g": f(inputs["subln_g"]).reshape(1, 128),
        "w_proj_sg": f(inputs["w_proj_sg"])[0],
        "w_proj_da": f(inputs["w_proj_da"])[0],
        "w_out": f(inputs["w_out"])[0],
        "norm2_g": f(inputs["norm2_g"]).reshape(1, D),
        "w_ffn_gate": f(inputs["w_ffn_gate"])[0],
        "w_ffn_up": f(inputs["w_ffn_up"])[0],
        "w_ffn_down": f(inputs["w_ffn_down"])[0],
    }
    consts = [make_consts(0), make_consts(1)]
    in_maps = []
    for c in range(8):
        b, half = divmod(c, 2)
        own = x[b, half * S_OWN:(half + 1) * S_OWN]
        oth = x[b, (1 - half) * S_OWN:(2 - half) * S_OWN]
        m = dict(shared)
        m["x"] = np.ascontiguousarray(np.concatenate([own, oth], axis=0))
        m.update(consts[half])
        in_maps.append(m)
    return in_maps


def kernel(**inputs):
    nc = build_program()
    in_maps = make_in_maps(inputs)
    res = run_bass_kernel_spmd(nc, in_maps, core_ids=list(range(8)))
    out = np.empty((4, S_ALL, D), np.float32)
    for c in range(8):
        b, half = divmod(c, 2)
        out[b, half * S_OWN:(half + 1) * S_OWN] = np.asarray(res.results[c]["out"], dtype=np.float32)
    return out
```
